# Optimizing a Trainium2 kernel written in Bass

```python
import jax, jax.numpy as jnp
from jax import lax
import numpy as np

D_MODEL = 2048
BATCH = 8
SEQ = 2048
DEPTH = 1
DEC_BATCH = 128
DEC_SEQ = 4
PAST_LEN = 2048
PAGE_SIZE = 128

MIX_WIDTH = D_MODEL
SB_HEADS = 8
SB_HEAD_DIM = MIX_WIDTH // (2 * SB_HEADS)
SB_WIDTH = SB_HEADS * SB_HEAD_DIM
HG_HEADS = 8
HG_KDIM = (MIX_WIDTH - SB_WIDTH) // HG_HEADS
HG_VDIM = HG_KDIM
HG_KWIDTH = HG_HEADS * HG_KDIM
HG_VWIDTH = HG_HEADS * HG_VDIM
PROJ_SPLITS = (SB_WIDTH, 2 * SB_WIDTH, 3 * SB_WIDTH,
               3 * SB_WIDTH + HG_KWIDTH,
               3 * SB_WIDTH + 2 * HG_KWIDTH,
               3 * SB_WIDTH + 2 * HG_KWIDTH + HG_VWIDTH)
PROJ_WIDTH = 3 * SB_WIDTH + 2 * HG_KWIDTH + 2 * HG_VWIDTH
D_FF = 4 * D_MODEL
SB_BLOCK = 128
SB_BIAS_INIT = -7.0
HG_CHUNK = 32
RMS_EPS = 1e-6

kernel_name = "hymba_stickbreaking_hgrn2_decode_step"


def rms_norm(x, g):
    x32 = x.astype(jnp.float32)
    return x32 * lax.rsqrt(jnp.mean(x32 * x32, axis=-1, keepdims=True) + RMS_EPS) * g.astype(jnp.float32)


def stick_breaking(q, k, v, q_pos, k_pos, bias):
    z = jnp.einsum('bthd,bnhd->bhtn', q, k) * (SB_HEAD_DIM ** -0.5) \
        + bias.astype(jnp.float32)[None, :, None, None]
    causal = (k_pos[None, :] < q_pos[:, None])[None, None]
    log_beta = jax.nn.log_sigmoid(z)
    log_keep = jnp.where(causal, jax.nn.log_sigmoid(-z), 0.0)
    rc = lax.cumsum(log_keep, axis=3, reverse=True)
    after = jnp.concatenate([rc[..., 1:], jnp.zeros_like(rc[..., :1])], axis=3)
    weights = jnp.where(causal, jnp.exp(log_beta + after), 0.0)
    return jnp.einsum('bhtn,bnhd->bthd', weights, v)


def sb_prompt(q, k, v, bias):
    B, S, H, D = q.shape
    nb = S // SB_BLOCK
    k_pos = jnp.arange(S)
    q_blocks = q.reshape(B, nb, SB_BLOCK, H, D).transpose(1, 0, 2, 3, 4)

    def block(args):
        qb, i = args
        return stick_breaking(qb, k, v, i * SB_BLOCK + jnp.arange(SB_BLOCK), k_pos, bias)

    o = lax.map(block, (q_blocks, jnp.arange(nb)))
    return o.transpose(1, 0, 2, 3, 4).reshape(B, S, H, D)


def hgrn2_recurrence(q, k, g, v, s0):
    B, L, H, dk = q.shape
    dv = v.shape[-1]
    c = HG_CHUNK if L % HG_CHUNK == 0 else L
    n = L // c

    def chunks(a):
        return a.reshape(B, n, c, H, a.shape[-1]).transpose(1, 0, 3, 2, 4)

    incl = jnp.tril(jnp.ones((c, c), dtype=bool))[None, None, :, :, None]

    def step(S, blk):
        qb, kb, gb, vb = blk
        b = jnp.cumsum(gb, axis=2)
        b_last = b[:, :, -1, :]
        o_inter = jnp.einsum('bhtk,bhkv->bhtv', qb * jnp.exp(b), S)
        decay = jnp.exp(jnp.where(incl, b[:, :, :, None, :] - b[:, :, None, :, :], -jnp.inf))
        scores = jnp.einsum('bhtk,bhsk,bhtsk->bhts', qb, kb, decay)
        o = o_inter + jnp.einsum('bhts,bhsv->bhtv', scores, vb)
        S = S * jnp.exp(b_last)[..., None] + jnp.einsum(
            'bhsk,bhsv->bhkv', kb * jnp.exp(b_last[:, :, None, :] - b), vb)
        return S, o

    S, o = lax.scan(step, s0, (chunks(q), chunks(k), chunks(g), chunks(v)))
    o = o.transpose(1, 0, 3, 2, 4).reshape(B, L, H, dv)
    return o, S


def split_projection(xn, w_in):
    proj = jnp.einsum('bsd,de->bse', xn, w_in.astype(jnp.float32))
    return jnp.split(proj, PROJ_SPLITS, axis=-1)


def sb_heads(a):
    B, L, _ = a.shape
    return a.reshape(B, L, SB_HEADS, SB_HEAD_DIM)


def hgrn_gates(hq, hf, hi, lb):
    B, L, _ = hq.shape
    q = jax.nn.silu(hq).reshape(B, L, HG_HEADS, HG_KDIM)
    f = lb + (1.0 - lb) * jax.nn.sigmoid(hf.reshape(B, L, HG_HEADS, HG_KDIM))
    return q, 1.0 - f, jnp.log(f), hi.reshape(B, L, HG_HEADS, HG_VDIM)


def merge_heads(sb_o, hg_o, hg_gate, sb_norm_g, hg_norm_g, w_out):
    B, L = sb_o.shape[:2]
    sb = rms_norm(sb_o, sb_norm_g.reshape(SB_HEADS, SB_HEAD_DIM)).reshape(B, L, SB_WIDTH)
    hg = rms_norm(hg_o, hg_norm_g.reshape(HG_HEADS, HG_VDIM)).reshape(B, L, HG_VWIDTH) * jax.nn.silu(hg_gate)
    return jnp.einsum('bse,ed->bsd', jnp.concatenate([sb, hg], axis=-1), w_out.astype(jnp.float32))


def channel_mlp(h, g, w_up, w_down):
    u = jnp.einsum('bsd,df->bsf', rms_norm(h, g), w_up.astype(jnp.float32))
    return h + jnp.einsum('bsf,fd->bsd', jnp.square(jax.nn.relu(u)), w_down.astype(jnp.float32))


def setup_inputs(seed: int = 0) -> dict:
    key = jax.random.key(seed)
    ks = jax.random.split(key, 20)
    n_pages = PAST_LEN // PAGE_SIZE
    n_used = DEC_BATCH * n_pages
    n_phys = n_used + max(1, n_used // 4)
    page_table = jax.random.permutation(ks[0], n_phys)[:n_used].reshape(DEC_BATCH, n_pages).astype(jnp.int32)
    nrm = jax.random.normal
    return {
        'x_prompt': nrm(ks[1], (BATCH, SEQ, D_MODEL), jnp.float32),
        'x_sample': nrm(ks[2], (DEC_BATCH, DEC_SEQ, D_MODEL), jnp.float32),
        'cache_k': nrm(ks[3], (DEPTH, n_phys, PAGE_SIZE, SB_HEADS, SB_HEAD_DIM), jnp.float32),
        'cache_v': nrm(ks[4], (DEPTH, n_phys, PAGE_SIZE, SB_HEADS, SB_HEAD_DIM), jnp.float32),
        'state_hgrn': 0.5 * nrm(ks[5], (DEPTH, DEC_BATCH, HG_HEADS, HG_KDIM, HG_VDIM), jnp.float32),
        'page_table': page_table,
        'norm1_g': 1.0 + 0.05 * nrm(ks[6], (DEPTH, D_MODEL), jnp.float32),
        'w_in': nrm(ks[7], (DEPTH, D_MODEL, PROJ_WIDTH), jnp.float32) * D_MODEL ** -0.5,
        'sb_bias': SB_BIAS_INIT + 0.1 * nrm(ks[16], (DEPTH, SB_HEADS), jnp.float32),
        'sb_norm_g': 1.0 + 0.05 * nrm(ks[8], (DEPTH, SB_WIDTH), jnp.float32),
        'hg_norm_g': 1.0 + 0.05 * nrm(ks[9], (DEPTH, HG_VWIDTH), jnp.float32),
        'hg_lb_logits': 0.5 * nrm(ks[10], (DEPTH + 1, HG_KWIDTH), jnp.float32),
        'w_out': nrm(ks[11], (DEPTH, MIX_WIDTH, D_MODEL), jnp.float32) * MIX_WIDTH ** -0.5,
        'norm2_g': 1.0 + 0.05 * nrm(ks[12], (DEPTH, D_MODEL), jnp.float32),
        'w_up': nrm(ks[13], (DEPTH, D_MODEL, D_FF), jnp.float32) * D_MODEL ** -0.5,
        'w_down': nrm(ks[14], (DEPTH, D_FF, D_MODEL), jnp.float32) * D_FF ** -0.5,
        'final_norm_g': 1.0 + 0.05 * nrm(ks[15], (D_MODEL,), jnp.float32),
    }


def reference(x_prompt, x_sample, cache_k, cache_v, state_hgrn, page_table, norm1_g, w_in, sb_bias,
              sb_norm_g, hg_norm_g, hg_lb_logits, w_out, norm2_g, w_up, w_down, final_norm_g):
    f32 = jnp.float32
    n_seq, n_pages = page_table.shape
    past_len = n_pages * cache_k.shape[2]
    B, S = x_prompt.shape[:2]
    t_new = x_sample.shape[1]
    q_pos_s = past_len + jnp.arange(t_new)
    k_pos_s = jnp.arange(past_len + t_new)
    lb_all = jnp.cumsum(jax.nn.softmax(hg_lb_logits.astype(f32), axis=0), axis=0)
    hp = x_prompt.astype(f32)
    hs = x_sample.astype(f32)
    kp_rows, vp_rows, sp_rows, ks_rows, vs_rows, ss_rows = [], [], [], [], [], []
    for l in range(DEPTH):
        lb = lb_all[l].reshape(HG_HEADS, HG_KDIM)
        sq, sk, sv, hq, hf, hi, hgate = split_projection(rms_norm(hp, norm1_g[l]), w_in[l])
        sq, sk, sv = sb_heads(sq), sb_heads(sk), sb_heads(sv)
        sb_o = sb_prompt(sq, sk, sv, sb_bias[l])
        gq, gk, gg, gv = hgrn_gates(hq, hf, hi, lb)
        hg_o, s_p = hgrn2_recurrence(gq, gk, gg, gv, jnp.zeros((B, HG_HEADS, HG_KDIM, HG_VDIM), f32))
        hp = hp + merge_heads(sb_o, hg_o, hgate, sb_norm_g[l], hg_norm_g[l], w_out[l])
        hp = channel_mlp(hp, norm2_g[l], w_up[l], w_down[l])
        kp_rows.append(sk.astype(cache_k.dtype))
        vp_rows.append(sv.astype(cache_v.dtype))
        sp_rows.append(s_p.astype(state_hgrn.dtype))
        tq, tk, tv, uq, uf, ui, ugate = split_projection(rms_norm(hs, norm1_g[l]), w_in[l])
        tq, tk, tv = sb_heads(tq), sb_heads(tk), sb_heads(tv)
        past_k = cache_k[l][page_table].reshape(n_seq, past_len, SB_HEADS, SB_HEAD_DIM).astype(f32)
        past_v = cache_v[l][page_table].reshape(n_seq, past_len, SB_HEADS, SB_HEAD_DIM).astype(f32)
        keys = jnp.concatenate([past_k, tk], axis=1)
        vals = jnp.concatenate([past_v, tv], axis=1)
        sb_o2 = stick_breaking(tq, keys, vals, q_pos_s, k_pos_s, sb_bias[l])
        gq2, gk2, gg2, gv2 = hgrn_gates(uq, uf, ui, lb)
        hg_o2, s_s = hgrn2_recurrence(gq2, gk2, gg2, gv2, state_hgrn[l].astype(f32))
        hs = hs + merge_heads(sb_o2, hg_o2, ugate, sb_norm_g[l], hg_norm_g[l], w_out[l])
        hs = channel_mlp(hs, norm2_g[l], w_up[l], w_down[l])
        ks_rows.append(tk.astype(cache_k.dtype))
        vs_rows.append(tv.astype(cache_v.dtype))
        ss_rows.append(s_s.astype(state_hgrn.dtype))
    y_prompt = rms_norm(hp, final_norm_g).astype(x_prompt.dtype)
    y_sample = rms_norm(hs, final_norm_g).astype(x_sample.dtype)
    new_k_prompt = jnp.stack(kp_rows)
    new_v_prompt = jnp.stack(vp_rows)
    new_state_prompt = jnp.stack(sp_rows)
    new_k_sample = jnp.stack(ks_rows)
    new_v_sample = jnp.stack(vs_rows)
    new_state_sample = jnp.stack(ss_rows)
    return (y_prompt, y_sample, new_k_prompt, new_v_prompt, new_state_prompt, new_k_sample, new_v_sample, new_state_sample)
```

```python
import contextlib
import numpy as np
import ml_dtypes
import concourse.bass as bass
import concourse.mybir as mybir
from concourse.bass_utils import run_bass_kernel_spmd

F32 = mybir.dt.float32
BF16 = mybir.dt.bfloat16
I32 = mybir.dt.int32
AF = mybir.ActivationFunctionType
ALU = mybir.AluOpType
AX = mybir.AxisListType

COMPUTE = ("pe", "act", "dve", "pool")
T = 2112
D = 2048
EPS = 1e-6
N_PHYS_ROWS = 2560 * 128


class Op:
    __slots__ = ("eng", "fn", "reads", "writes", "dsem", "deps", "signal", "sigval", "idx", "is_dma")

    def __init__(self, eng, fn, reads, writes, dsem):
        self.eng = eng
        self.fn = fn
        self.reads = reads
        self.writes = writes
        self.dsem = dsem
        self.is_dma = dsem is not None
        self.deps = []
        self.signal = False
        self.sigval = 0


class Prog:
    def __init__(self, nc):
        self.nc = nc
        self.ops = []
        self.last_w = {}
        self.readers = {}
        self.bar_start = 0

    def op(self, eng, fn, reads=(), writes=(), dsem=None):
        o = Op(eng, fn, tuple(reads), tuple(writes), dsem)
        o.idx = len(self.ops)
        deps = set()
        raw = set()
        for r in o.reads:
            w = self.last_w.get(r)
            if w is not None:
                deps.add(w)
                raw.add(w)
        for wk in o.writes:
            w = self.last_w.get(wk)
            if w is not None:
                deps.add(w)
            for rd in self.readers.get(wk, ()):
                deps.add(rd)
        fdeps = []
        for d in deps:
            po = self.ops[d]
            if (not po.is_dma) and (not o.is_dma) and po.eng == o.eng and o.eng == "pe":
                continue
            fdeps.append(d)
        o.deps = fdeps
        self.ops.append(o)
        for r in o.reads:
            self.readers.setdefault(r, []).append(o.idx)
        for wk in o.writes:
            self.last_w[wk] = o.idx
            self.readers[wk] = []
        return o

    def dma(self, q, out, in_, reads, writes, sem, **kw):
        return self.op(q, lambda e: e.dma_start(out=out, in_=in_, **kw), reads, writes, dsem=sem)

    def barrier(self):
        last = {}
        dmas = {}
        for o in self.ops[self.bar_start:]:
            if o.fn is None:
                continue
            if o.is_dma:
                dmas[o.dsem] = o.idx
            else:
                last[o.eng] = o.idx
        deps = list(dmas.values()) + list(last.values())
        for eng in ("pe", "act", "dve", "pool", "sp"):
            o = Op(eng, None, (), (), None)
            o.idx = len(self.ops)
            o.deps = list(deps)
            self.ops.append(o)
        self.bar_start = len(self.ops)
        self.last_w.clear()
        self.readers.clear()

    def emit(self):
        nc = self.nc
        ops = self.ops
        for o in ops:
            best = {}
            keep = []
            for d in o.deps:
                po = ops[d]
                if po.is_dma:
                    keep.append(d)
                else:
                    if po.eng not in best or best[po.eng] < d:
                        best[po.eng] = d
            o.deps = keep + list(best.values())
            for d in o.deps:
                ops[d].signal = True
        cnt = {e: 0 for e in COMPUTE}
        dcnt = {}
        for o in ops:
            if o.is_dma:
                dcnt[o.dsem] = dcnt.get(o.dsem, 0) + 1
                o.sigval = 16 * dcnt[o.dsem]
            elif o.signal:
                cnt[o.eng] += 1
                o.sigval = cnt[o.eng]
        stack = contextlib.ExitStack()
        sems = {}
        for e in COMPUTE:
            sems[e] = stack.enter_context(nc.semaphore("s_" + e))
        for k in dcnt:
            sems[("d", k)] = stack.enter_context(nc.semaphore("d%d" % len(sems)))
        self.n_sems = len(sems)
        engmap = {"pe": nc.tensor, "act": nc.scalar, "dve": nc.vector, "pool": nc.gpsimd, "sp": nc.sync}
        by_eng = {e: [] for e in engmap}
        for o in ops:
            by_eng[o.eng].append(o)

        def run_engine(ename, eh):
            known = {}
            for o in by_eng[ename]:
                need = {}
                for d in o.deps:
                    po = ops[d]
                    sk = ("d", po.dsem) if po.is_dma else po.eng
                    if need.get(sk, 0) < po.sigval:
                        need[sk] = po.sigval
                for sk, v in need.items():
                    if known.get(sk, 0) >= v:
                        continue
                    known[sk] = v
                    eh.wait_ge(sems[sk], v)
                if o.fn is None:
                    continue
                ins = o.fn(eh)
                if o.is_dma:
                    ins.then_inc(sems[("d", o.dsem)], 16)
                elif o.signal:
                    ins.then_inc(sems[o.eng], 1)

        with nc.Block() as block:
            @block.sync
            def _(e):
                run_engine("sp", e)

            @block.tensor
            def _(e):
                run_engine("pe", e)

            @block.scalar
            def _(e):
                run_engine("act", e)

            @block.vector
            def _(e):
                run_engine("dve", e)

            @block.gpsimd
            def _(e):
                run_engine("pool", e)
        stack.close()


def build_program(n_cache_rows=N_PHYS_ROWS, stop=None, nheads=8):
    nc = bass.Bass("TRN2", target_bir_lowering=False)
    P = Prog(nc)

    def din(name, shape, dt=F32):
        return nc.dram_tensor(name, list(shape), dt, kind="ExternalInput").ap()

    def dout(name, shape, dt=F32):
        return nc.dram_tensor(name, list(shape), dt, kind="ExternalOutput").ap()

    x = din("x", [T, D])
    ck = din("ck", [n_cache_rows, 1024])
    cv = din("cv", [n_cache_rows, 1024])
    st_in = din("st", [128, 128, 128])
    pt = din("pt", [1, 256], I32)
    w_in = din("w_in", [D, 7168])
    w_out = din("w_out", [D, D])
    w_up = din("w_up", [D, 8192])
    w_down = din("w_down", [8192, D])
    n1g = din("n1g", [1, D])
    n2g = din("n2g", [1, D])
    fg = din("fg", [1, D])
    sbb = din("sbb", [1, 8])
    sbg = din("sbg", [1, 1024])
    hgg = din("hgg", [1, 1024])
    lbl = din("lbl", [2, 1024])
    c_ident = din("c_ident", [128, 128], BF16)
    c_triu = din("c_triu", [128, 128], BF16)
    c_omt = din("c_omt", [128, 128], BF16)
    c_ones = din("c_ones", [128, 128], BF16)
    c_mstrict = din("c_mstrict", [128, 128], BF16)
    c_mincl = din("c_mincl", [64, 64], BF16)
    c_scan = din("c_scan", [1, T])
    c_m4 = din("c_m4", [4, 32])
    c_hsel = din("c_hsel", [8, 512])

    y = dout("y", [T, D])
    nk = dout("nk", [T, 1024])
    nv = dout("nv", [T, 1024])
    nsp = dout("nsp", [8, 128, 128])
    nss = dout("nss", [128, 128, 128])
    mixT = nc.dram_tensor("mixT_scr", [D, T], BF16).ap()

    BLKS = [(i * 128, 128) for i in range(16)] + [(2048, 64)]
    TG = [(0, 512), (512, 512), (1024, 512), (1536, 512), (2048, 64)]

    S_all = contextlib.ExitStack()

    def sb(stack, name, shape, dt):
        return stack.enter_context(nc.sbuf_tensor(name, list(shape), dt))

    banks = [S_all.enter_context(nc.psum_tensor("bank%d" % i, [128, 512], F32)) for i in range(8)]

    def bk(i):
        return banks[i]

    def bkbf(i):
        return banks[i][:].bitcast(BF16)

    ident = sb(S_all, "ident", [128, 128], BF16)
    triu = sb(S_all, "triu", [128, 128], BF16)
    omt = sb(S_all, "omt", [128, 128], BF16)
    ones = sb(S_all, "ones", [128, 128], BF16)
    mstrict = sb(S_all, "mstrict", [128, 128], BF16)
    mincl = sb(S_all, "mincl", [64, 64], BF16)
    m4 = sb(S_all, "m4", [4, 32], F32)
    hsel = sb(S_all, "hsel", [8, 512], BF16)
    hsel32 = sb(S_all, "hsel32", [8, 512], F32)
    biasc = sb(S_all, "biasc", [128, 8], F32)
    bias8 = sb(S_all, "bias8", [8, 1], F32)
    hselb = sb(S_all, "hselb", [8, 512], F32)
    ones32 = sb(S_all, "ones32", [8, 128], F32)
    gsbc = sb(S_all, "gsbc", [128, 8], F32)
    ghgc = sb(S_all, "ghgc", [128, 8], F32)
    lbt = sb(S_all, "lbt", [128, 2, 8], F32)
    lbc = sb(S_all, "lbc", [128, 8], F32)
    omlc = sb(S_all, "omlc", [128, 8], F32)
    sscol = sb(S_all, "sscol", [128, 4], F32)
    QTs = sb(S_all, "QTs", [128, 8, 64], BF16)
    KTs = sb(S_all, "KTs", [128, 8, 64], BF16)
    Vs_all = sb(S_all, "Vs_all", [64, 8, 128], BF16)

    def pbc(ap2d, n):
        return ap2d.partition_broadcast(n).rearrange("p o n -> p (o n)")

    def ld(dst, src, key, **kw):
        P.dma("sp", dst, src, reads=[], writes=[key], sem=key, **kw)

    ld(ident[:], c_ident, "ident")
    ld(triu[:], c_triu, "triu")
    ld(omt[:], c_omt, "omt")
    ld(ones[:], c_ones, "ones")
    ld(mstrict[:], c_mstrict, "mstrict")
    ld(mincl[:], c_mincl, "mincl")
    ld(m4[:], c_m4, "m4")
    ld(hsel32[:], c_hsel, "hsel32")
    ld(biasc[:], pbc(sbb, 128), "biasc")
    ld(bias8[:], sbb.rearrange("o h -> h o"), "bias8", allow_slow_non_contiguous=True)
    ld(gsbc[:], sbg.rearrange("o (h d) -> d (o h)", d=128), "gsbc", allow_slow_non_contiguous=True)
    ld(ghgc[:], hgg.rearrange("o (h d) -> d (o h)", d=128), "ghgc", allow_slow_non_contiguous=True)
    ld(lbt[:], lbl.rearrange("r (h d) -> d r h", d=128), "lbt", allow_slow_non_contiguous=True)
    P.op("dve", lambda e: e.tensor_copy(out=hsel[:], in_=hsel32[:]), ["hsel32"], ["hsel"])
    P.op("dve", lambda e: e.tensor_scalar(out=hselb[:], in0=hsel32[:], scalar1=bias8[:, 0:1], scalar2=None, op0=ALU.mult), ["hsel32", "bias8"], ["hselb"])
    P.op("dve", lambda e: e.memset(ones32[:], 1.0), [], ["ones32"])
    P.op("dve", lambda e: e.tensor_tensor(out=lbc[:], in0=lbt[:, 1, :], in1=lbt[:, 0, :], op=ALU.subtract), ["lbt"], ["lbc"])
    P.op("act", lambda e: e.activation(out=lbc[:], in_=lbc[:], func=AF.Exp), ["lbc"], ["lbc"])
    P.op("dve", lambda e: e.tensor_scalar(out=lbc[:], in0=lbc[:], scalar1=1.0, scalar2=None, op0=ALU.add), ["lbc"], ["lbc"])
    P.op("dve", lambda e: e.reciprocal(out=lbc[:], in_=lbc[:]), ["lbc"], ["lbc"])
    P.op("dve", lambda e: e.tensor_scalar(out=omlc[:], in0=lbc[:], scalar1=-1.0, scalar2=1.0, op0=ALU.mult, op1=ALU.add), ["lbc"], ["omlc"])

    if stop == "p0":
        P.barrier()
        P.emit()
        return nc

    def rstd_col(n, src_key, dst, scale):
        P.op("act", lambda e: e.activation(out=sscol[:n, 1:2], in_=sscol[:n, 0:1], func=AF.Ln, bias=EPS, scale=scale), [src_key], ["ss1"])
        P.op("act", lambda e: e.activation(out=dst, in_=sscol[:n, 1:2], func=AF.Exp, scale=-0.5), ["ss1"], ["rstdc"])

    S_p1 = contextlib.ExitStack()
    xnT = sb(S_p1, "xnT", [128, 16, T], BF16)

    def norm_transpose(src_tile, n, gbc, dstT, c0, junk, xs, src_key, tb0, dkey="xT"):
        P.op("act", lambda e: e.activation(out=junk[:n, :], in_=src_tile, func=AF.Square, accum_out=sscol[:n, 0:1]), [src_key], ["junk", "ss0"])
        import os as _os
        dbg = int(_os.environ.get("DBG", "9"))
        if dbg < 2:
            return
        rstd_col(n, "ss0", sscol[:n, 2:3], 1.0 / D)
        if dbg < 3:
            return
        P.op("dve", lambda e: e.scalar_tensor_tensor(out=xs[:n, :], in0=src_tile, scalar=sscol[:n, 2:3], in1=gbc[:n, :], op0=ALU.mult, op1=ALU.mult), [src_key, "rstdc", "gbc"], ["xs"])
        if dbg < 4:
            return
        for half in range(2):
            bnk = tb0 + half
            pv = bkbf(bnk)
            for cc in range(8):
                c = half * 8 + cc
                P.op("pe", lambda e, c=c, cc=cc, pv=pv: e.transpose(out=pv[:, cc * 128:cc * 128 + n], in_=xs[:n, c * 128:(c + 1) * 128], identity=ident[:n, :n]), ["xs", "ident"], [("bank", bnk)])
            if dbg < 5:
                continue
            if dbg == 5 and half == 1:
                continue
            if dbg == 6 and half == 0:
                continue
            src = pv[:, :].rearrange("p (c t) -> p c t", c=8)[:, :, :n]
            dst = dstT[:, half * 8:half * 8 + 8, c0:c0 + n]
            if half == 0 or dbg == 7:
                P.op("act", lambda e, src=src, dst=dst: e.activation(out=dst, in_=src, func=AF.Copy), [("bank", bnk)], [(dkey, half)])
            else:
                P.op("dve", lambda e, src=src, dst=dst: e.tensor_copy(out=dst, in_=src), [("bank", bnk)], [(dkey, half)])

    S_p1a = contextlib.ExitStack()
    g1bc = sb(S_p1a, "g1bc", [128, D], F32)
    xin = [sb(S_p1a, "xin%d" % i, [128, D], F32) for i in range(2)]
    junk = sb(S_p1a, "junk", [128, D], BF16)
    xs = sb(S_p1a, "xs", [128, D], BF16)
    P.dma("sp", g1bc[:], pbc(n1g, 128), [], ["gbc"], "gbc")
    for bi, (r0, n) in enumerate(BLKS):
        xb = xin[bi % 2]
        P.dma("sp", xb[:n, :], x[r0:r0 + n, :], [], [("xin", bi % 2)], ("xin", bi % 2))
        norm_transpose(xb[:n, :], n, g1bc, xnT, r0, junk, xs, ("xin", bi % 2), 2)
    P.barrier()
    S_p1a.close()
    if stop == "p1a":
        P.emit()
        return nc

    S_p1b = contextlib.ExitStack()
    NU = 8
    wring = sb(S_p1b, "wring", [128, NU, 16, 128], BF16)
    ring_cnt = [0]

    def load_w_unit(col0):
        u = ring_cnt[0] % NU
        ring_cnt[0] += 1
        src = w_in[:, col0:col0 + 128].rearrange("(c p) n -> p c n", p=128)
        P.dma("pool", wring[:, u, :, :], src, [], [("wr", u)], ("wr", u))
        return u

    QT = sb(S_p1b, "QT", [128, T], BF16)
    KT = sb(S_p1b, "KT", [128, T], BF16)
    Kb = sb(S_p1b, "Kb", [128, 17, 128], BF16)
    Vb = sb(S_p1b, "Vb", [128, 17, 128], BF16)
    kvst = [sb(S_p1b, "kvst%d" % i, [128, 256], F32) for i in range(2)]
    e_t = [sb(S_p1b, "e_t%d" % i, [128, 512], F32) for i in range(2)]
    spb_t = [sb(S_p1b, "spb%d" % i, [128, 512], BF16) for i in range(2)]
    t1_t = [sb(S_p1b, "t1_%d" % i, [128, 512], F32) for i in range(2)]
    s_t = [sb(S_p1b, "s_%d" % i, [128, 512], F32) for i in range(2)]
    W_t = [sb(S_p1b, "W_%d" % i, [128, 512], BF16) for i in range(2)]
    Osb = sb(S_p1b, "Osb", [128, 512], F32)
    sq_t = sb(S_p1b, "sq_t", [128, 512], BF16)
    rs_t = sb(S_p1b, "rs_t", [128, 512], F32)
    mix_t = sb(S_p1b, "mix_t", [128, 512], BF16)
    hA = sb(S_p1b, "hA", [128, 512], F32)
    hB = sb(S_p1b, "hB", [128, 512], F32)
    hC = sb(S_p1b, "hC", [128, 512], F32)
    hD = sb(S_p1b, "hD", [128, 512], F32)
    hE = sb(S_p1b, "hE", [128, 512], F32)
    scanm = sb(S_p1b, "scanm", [128, T], BF16)
    scan32 = sb(S_p1b, "scan32", [128, 512], F32)
    qe = sb(S_p1b, "qe", [128, T], BF16)
    ke = sb(S_p1b, "ke", [128, T], BF16)
    sg = sb(S_p1b, "sg", [128, T], BF16)
    Vc = sb(S_p1b, "Vc", [64, 33, 128], BF16)
    Vs4 = sb(S_p1b, "Vs4", [4, 16, 128], BF16)
    dl = sb(S_p1b, "dl", [128, 48], F32)
    STm = [sb(S_p1b, "STm%d" % i, [64, 64], BF16) for i in range(2)]
    keTok = [sb(S_p1b, "keTok%d" % i, [64, 128], BF16) for i in range(2)]
    Sst = [sb(S_p1b, "Sst%d" % i, [128, 128], F32) for i in range(2)]
    S1 = sb(S_p1b, "S1", [128, 128], F32)
    Sbf = [sb(S_p1b, "Sbf%d" % i, [128, 128], BF16) for i in range(2)]
    Sld = [sb(S_p1b, "Sld%d" % i, [128, 128], F32) for i in range(2)]
    hOsb = sb(S_p1b, "hOsb", [128, 256], F32)
    hsq = sb(S_p1b, "hsq", [128, 256], BF16)
    hrs = sb(S_p1b, "hrs", [128, 256], F32)
    hm1 = sb(S_p1b, "hm1", [128, 256], F32)
    hmix = sb(S_p1b, "hmix", [128, 256], BF16)

    for gi, (t0, n) in enumerate(TG):
        P.dma("sp", scan32[:, :n], pbc(c_scan[:, t0:t0 + n], 128), [], ["scan32"], "scan32")
        P.op("dve", lambda e, t0=t0, n=n: e.tensor_copy(out=scanm[:, t0:t0 + n], in_=scan32[:, :n]), ["scan32"], ["scanm"])

    INV_SQRT_D = 128.0 ** -0.5

    def proj_fm(u, t0, n, bnk):
        for c in range(16):
            P.op("pe", lambda e, c=c: e.matmul(bk(bnk)[:, :n], lhsT=wring[:, u, c, :], rhs=xnT[:, c, t0:t0 + n], start=(c == 0), stop=(c == 15)),
                 [("wr", u)], [("bank", bnk)])

    def sb_head(h):
        uq = load_w_unit(h * 128)
        if ring_cnt[0] % NU == NU - 1:
            ring_cnt[0] += 1
        uk = load_w_unit(1024 + h * 128)
        uv = load_w_unit(2048 + h * 128)
        assert uv == uk + 1
        for gi, (t0, n) in enumerate(TG):
            bnk = gi % 2
            proj_fm(uq, t0, n, bnk)
            P.op("act", lambda e, t0=t0, n=n, bnk=bnk: e.activation(out=QT[:, t0:t0 + n], in_=bk(bnk)[:, :n], func=AF.Copy, scale=INV_SQRT_D), [("bank", bnk)], ["QT"])
            if gi == 4:
                P.op("dve", lambda e, bnk=bnk: e.tensor_scalar(out=QTs[:, h, :], in0=bk(bnk)[:, :64], scalar1=INV_SQRT_D, scalar2=None, op0=ALU.mult), [("bank", bnk)], ["QTs"])
        for bi, (r0, n) in enumerate(BLKS):
            bnk = bi % 2
            for c in range(16):
                P.op("pe", lambda e, c=c, r0=r0, n=n, bnk=bnk: e.matmul(bk(bnk)[:n, 0:256].rearrange("p (u n) -> p u n", u=2), lhsT=xnT[:, c, r0:r0 + n], rhs=wring[:, uk:uk + 2, c, :], start=(c == 0), stop=(c == 15)),
                     [("wr", uk), ("wr", uv)], [("bank", bnk)])
            kp = bi % 2
            P.op("act", lambda e, n=n, bnk=bnk, kp=kp: e.activation(out=kvst[kp][:n, :], in_=bk(bnk)[:n, 0:256], func=AF.Copy), [("bank", bnk)], [("kvst", kp)])
            P.dma("sp", nk[r0:r0 + n, h * 128:(h + 1) * 128], kvst[kp][:n, 0:128], [("kvst", kp)], [], ("nk", kp))
            P.dma("sp", nv[r0:r0 + n, h * 128:(h + 1) * 128], kvst[kp][:n, 128:256], [("kvst", kp)], [], ("nv", kp))
            P.op("dve", lambda e, n=n, bi=bi, kp=kp: e.tensor_copy(out=Kb[:n, bi, :], in_=kvst[kp][:n, 0:128]), [("kvst", kp)], ["Kb"])
            P.op("dve", lambda e, n=n, bi=bi, kp=kp: e.tensor_copy(out=Vb[:n, bi, :], in_=kvst[kp][:n, 128:256]), [("kvst", kp)], ["Vb"])
            if bi == 16:
                P.op("dve", lambda e, kp=kp: e.tensor_copy(out=Vs_all[:64, h, :], in_=kvst[kp][:64, 128:256]), [("kvst", kp)], ["Vs_all"])
        pv = bkbf(2)
        for g0 in range(0, 17, 8):
            blks = list(range(g0, min(g0 + 8, 17)))
            for bi in blks:
                r0, n = BLKS[bi]
                P.op("pe", lambda e, bi=bi, n=n, g0=g0: e.transpose(out=pv[:, (bi - g0) * 128:(bi - g0) * 128 + n], in_=Kb[:n, bi, :], identity=ident[:n, :n]), ["Kb", "ident"], [("bank", 2)])
            if g0 < 16:
                P.op("dve", lambda e, g0=g0: e.tensor_copy(out=KT[:, g0 * 128:g0 * 128 + 1024], in_=pv[:, 0:1024]), [("bank", 2)], ["KT"])
            else:
                P.op("dve", lambda e: e.tensor_copy(out=KT[:, 2048:2112], in_=pv[:, 0:64]), [("bank", 2)], ["KT"])
                P.op("dve", lambda e: e.tensor_copy(out=KTs[:, h, :], in_=pv[:, 0:64]), [("bank", 2)], ["KTs"])
        yield
        bh = biasc[:, h:h + 1]
        step = 0
        for qg in range(4):
            q0 = qg * 512
            blocks = list(range(4 * qg + 3, -1, -1))
            nb = len(blocks)
            base = step

            def prm(idx, qg=qg, blocks=blocks, base=base):
                kb = blocks[idx]
                j = kb - 4 * qg
                c0 = 128 * j if j >= 0 else 0
                par = (base + idx) % 2
                return kb, j, c0, par

            def front(idx, q0=q0):
                kb, j, c0, par = prm(idx)
                zb = 3 + par
                Z = bk(zb)
                P.op("pe", lambda e: e.matmul(Z[:, c0:512], lhsT=KT[:, kb * 128:(kb + 1) * 128], rhs=QT[:, q0 + c0:q0 + 512], start=True, stop=True),
                     ["KT", "QT"], [("bank", zb)])
                P.op("act", lambda e: e.activation(out=e_t[par][:, c0:], in_=Z[:, c0:512], func=AF.Exp, bias=bh), [("bank", zb), "biasc"], [("e", par)])
                P.op("act", lambda e: e.activation(out=spb_t[par][:, c0:], in_=e_t[par][:, c0:], func=AF.Ln, bias=1.0), [("e", par)], [("spb", par)])
                if j >= 0:
                    P.op("pool", lambda e: e.tensor_tensor(out=spb_t[par][:, c0:c0 + 128], in0=spb_t[par][:, c0:c0 + 128], in1=mstrict[:], op=ALU.mult), [("spb", par), "mstrict"], [("spb", par)])
                P.op("dve", lambda e: e.tensor_tensor(out=t1_t[par][:, c0:], in0=Z[:, c0:512], in1=spb_t[par][:, c0:], op=ALU.subtract), [("bank", zb), ("spb", par)], [("t1", par)])

            def mid(idx):
                kb, j, c0, par = prm(idx)
                P.op("pe", lambda e: e.matmul(bk(5)[:, c0:512], lhsT=triu[:], rhs=spb_t[par][:, c0:], start=(idx == 0), stop=True, skip_group_check=True),
                     [("spb", par), "triu"], [("bank", 5)])
                P.op("dve", lambda e: e.tensor_tensor(out=s_t[par][:, c0:], in0=t1_t[par][:, c0:], in1=bk(5)[:, c0:512], op=ALU.subtract), [("t1", par), ("bank", 5)], [("s", par)])
                P.op("act", lambda e: e.activation(out=W_t[par][:, c0:], in_=s_t[par][:, c0:], func=AF.Exp, bias=bh), [("s", par), "biasc"], [("W", par)])
                if j >= 0:
                    P.op("pool", lambda e: e.tensor_tensor(out=W_t[par][:, c0:c0 + 128], in0=W_t[par][:, c0:c0 + 128], in1=mstrict[:], op=ALU.mult), [("W", par), "mstrict"], [("W", par)])

            def back(idx, nb=nb):
                kb, j, c0, par = prm(idx)
                if idx < nb - 1:
                    P.op("pe", lambda e: e.matmul(bk(5)[:, c0:512], lhsT=omt[:], rhs=spb_t[par][:, c0:], start=False, stop=True, skip_group_check=True),
                         [("spb", par), "omt"], [("bank", 5)])
                P.op("pe", lambda e: e.matmul(bk(6)[:, c0:512], lhsT=Vb[:, kb, :], rhs=W_t[par][:, c0:], start=(idx == 0), stop=(idx == nb - 1), skip_group_check=True),
                     [("W", par), "Vb"], [("bank", 6)])

            front(0)
            for idx in range(nb):
                if idx + 1 < nb:
                    front(idx + 1)
                mid(idx)
                yield
                back(idx)
            step += nb
            P.op("act", lambda e: e.activation(out=Osb[:], in_=bk(6)[:, :], func=AF.Copy), [("bank", 6)], ["Osb"])
            P.op("act", lambda e: e.activation(out=sq_t[:], in_=Osb[:], func=AF.Square), ["Osb"], ["sq"])
            P.op("pe", lambda e: e.matmul(bk(2)[:, :], lhsT=ones[:], rhs=sq_t[:], start=True, stop=True), ["sq", "ones"], [("bank", 2)])
            P.op("act", lambda e: e.activation(out=rs_t[:], in_=bk(2)[:, :], func=AF.Ln, bias=EPS, scale=1.0 / 128), [("bank", 2)], ["rs"])
            P.op("act", lambda e: e.activation(out=rs_t[:], in_=rs_t[:], func=AF.Exp, scale=-0.5), ["rs"], ["rs"])
            P.op("dve", lambda e: e.scalar_tensor_tensor(out=mix_t[:], in0=Osb[:], scalar=gsbc[:, h:h + 1], in1=rs_t[:], op0=ALU.mult, op1=ALU.mult), ["Osb", "rs", "gsbc"], ["mix"])
            P.dma("sp", mixT[h * 128:(h + 1) * 128, q0:q0 + 512], mix_t[:], ["mix"], [], "mixo")
            yield

    def silu_from_bank(bnk, n, out_ap, out_key):
        P.op("act", lambda e: e.activation(out=hD[:, :n], in_=bk(bnk)[:, :n], func=AF.Exp, scale=-1.0), [("bank", bnk)], ["hD"])
        P.op("act", lambda e: e.activation(out=hE[:, :n], in_=bk(bnk)[:, :n], func=AF.Copy), [("bank", bnk)], ["hE"])
        P.op("dve", lambda e: e.tensor_scalar(out=hD[:, :n], in0=hD[:, :n], scalar1=1.0, scalar2=None, op0=ALU.add), ["hD"], ["hD"])
        P.op("dve", lambda e: e.reciprocal(out=hD[:, :n], in_=hD[:, :n]), ["hD"], ["hD"])
        P.op("dve", lambda e: e.tensor_tensor(out=out_ap, in0=hE[:, :n], in1=hD[:, :n], op=ALU.mult), ["hD", "hE"], [out_key])

    def hg_head(h):
        uq = load_w_unit(3072 + h * 128)
        uf = load_w_unit(4096 + h * 128)
        ui = load_w_unit(5120 + h * 128)
        ug = load_w_unit(6144 + h * 128)
        for gi, (t0, n) in enumerate(TG):
            proj_fm(uq, t0, n, 0)
            silu_from_bank(0, n, hC[:, :n], "hC")
            proj_fm(uf, t0, n, 1)
            P.op("act", lambda e, n=n: e.activation(out=hA[:, :n], in_=bk(1)[:, :n], func=AF.Exp, scale=-1.0), [("bank", 1)], ["hA"])
            P.op("dve", lambda e, n=n: e.tensor_scalar(out=hA[:, :n], in0=hA[:, :n], scalar1=1.0, scalar2=None, op0=ALU.add), ["hA"], ["hA"])
            P.op("dve", lambda e, n=n: e.reciprocal(out=hA[:, :n], in_=hA[:, :n]), ["hA"], ["hA"])
            P.op("dve", lambda e, n=n: e.tensor_scalar(out=hA[:, :n], in0=hA[:, :n], scalar1=omlc[:, h:h + 1], scalar2=lbc[:, h:h + 1], op0=ALU.mult, op1=ALU.add), ["hA", "omlc", "lbc"], ["hA"])
            P.op("act", lambda e, n=n: e.activation(out=hB[:, :n], in_=hA[:, :n], func=AF.Ln), ["hA"], ["hB"])
            P.op("dve", lambda e, n=n: e.tensor_scalar(out=hA[:, :n], in0=hA[:, :n], scalar1=-1.0, scalar2=1.0, op0=ALU.mult, op1=ALU.add), ["hA"], ["hA"])
            P.op("dve", lambda e, n=n, t0=t0: e.tensor_tensor_scan(out=hD[:, :n], data0=scanm[:, t0:t0 + n], data1=hB[:, :n], initial=0.0, op0=ALU.mult, op1=ALU.add), ["hB", "scanm"], ["hD"])
            P.op("act", lambda e, n=n: e.activation(out=hB[:, :n], in_=hD[:, :n], func=AF.Exp), ["hD"], ["hB"])
            P.op("dve", lambda e, n=n, t0=t0: e.tensor_tensor(out=qe[:, t0:t0 + n], in0=hC[:, :n], in1=hB[:, :n], op=ALU.mult), ["hC", "hB"], ["qe"])
            if gi < 4:
                P.op("dve", lambda e, gi=gi, n=n: e.tensor_copy(out=dl[:, gi * 8:gi * 8 + 8], in_=hB[:, :n].rearrange("p (c l) -> p c l", l=64)[:, :, 63]), ["hB"], ["dl"])
            else:
                P.op("dve", lambda e, n=n: e.tensor_copy(out=dl[:, 32:48], in_=hB[:, :n].rearrange("p (c l) -> p c l", l=4)[:, :, 3]), ["hB"], ["dl"])
            P.op("act", lambda e, n=n: e.activation(out=hB[:, :n], in_=hD[:, :n], func=AF.Exp, scale=-1.0), ["hD"], ["hB"])
            P.op("dve", lambda e, n=n, t0=t0: e.tensor_tensor(out=ke[:, t0:t0 + n], in0=hA[:, :n], in1=hB[:, :n], op=ALU.mult), ["hA", "hB"], ["ke"])
            proj_fm(ug, t0, n, 0)
            silu_from_bank(0, n, sg[:, t0:t0 + n], "sg")
        for ci in range(33):
            r0 = 64 * ci
            bnk = ci % 2
            for c in range(16):
                P.op("pe", lambda e, c=c, r0=r0, bnk=bnk: e.matmul(bk(bnk)[:64, 0:128], lhsT=xnT[:, c, r0:r0 + 64], rhs=wring[:, ui, c, :], start=(c == 0), stop=(c == 15)),
                     [("wr", ui)], [("bank", bnk)])
            if True:
                P.op("act", lambda e, ci=ci, bnk=bnk: e.activation(out=Vc[:64, ci, :], in_=bk(bnk)[:64, 0:128], func=AF.Copy), [("bank", bnk)], ["Vc"])
            else:
                P.op("dve", lambda e, ci=ci, bnk=bnk: e.tensor_copy(out=Vc[:64, ci, :], in_=bk(bnk)[:64, 0:128]), [("bank", bnk)], ["Vc"])
        for b in range(16):
            P.dma("sp", Vs4[0:4, b, :], Vc[4 * b:4 * b + 4, 32, :], ["Vc"], ["Vs4"], ("vs4", b % 4))
        yield
        import os as _os
        hgdbg = int(_os.environ.get("HGDBG", "9"))
        if hgdbg < 1:
            return
        P.op("dve", lambda e: e.memset(Sst[0][:], 0.0), [], [("S", 0)])
        P.op("dve", lambda e: e.memset(Sbf[0][:], 0.0), [], [("Sbf", 0)])
        chunks = [("p", ci, 64 * ci, 64, ci) for ci in range(32)] + [("s", b, 2048 + 4 * b, 4, 32 + b) for b in range(16)]
        if hgdbg >= 1 and hgdbg < 9:
            chunks = chunks[:32]
        cur = 0
        hb = bk(7)
        hbb = bkbf(7)
        for stp, (kind, ci, tok0, L, di) in enumerate(chunks):
            par = stp % 2
            P.op("pe", lambda e, tok0=tok0, L=L: e.matmul(bk(0)[:L, 0:L], lhsT=ke[:, tok0:tok0 + L], rhs=qe[:, tok0:tok0 + L], start=True, stop=True), ["ke", "qe"], [("bank", 0)])
            P.op("pe", lambda e, tok0=tok0, L=L: e.transpose(out=bkbf(0)[:L, 640:768], in_=ke[:, tok0:tok0 + L], identity=ident[:, :]), ["ke", "ident"], [("bank", 0)])
            if hgdbg == 2:
                yield
                continue
            if _os.environ.get("HG3") != "act":
                P.op("dve", lambda e, L=L, par=par: e.tensor_tensor(out=STm[par][:L, :L], in0=bk(0)[:L, 0:L], in1=mincl[:L, :L], op=ALU.mult), [("bank", 0), "mincl"], [("STm", par)])

            if _os.environ.get("HG3") != "dve":
                P.op("dve", lambda e, L=L, par=par: e.tensor_copy(out=keTok[par][:L, :], in_=bkbf(0)[:L, 640:768]), [("bank", 0)], [("keTok", par)])

            if hgdbg == 3:
                yield
                continue
            if kind == "s":
                lp = ci % 2
                P.dma("sp", Sld[lp][:], st_in[ci * 8 + h, :, :], [], [("Sld", lp)], ("Sld", lp))
                S_cur = Sld[lp]
                S_key = ("Sld", lp)
                P.op("dve", lambda e, S_cur=S_cur, cur=cur: e.tensor_copy(out=Sbf[cur][:], in_=S_cur[:]), [S_key], [("Sbf", cur)])
                Vcur = Vs4[0:4, ci, :]
                vkey = "Vs4"
                col = 4 * ci
            else:
                S_cur = Sst[cur]
                S_key = ("S", cur)
                Vcur = Vc[:64, ci, :]
                vkey = "Vc"
                col = (ci % 4) * 64
            P.op("pe", lambda e, L=L, par=par, Vcur=Vcur, col=col: e.matmul(hb[:, col:col + L], lhsT=Vcur, rhs=STm[par][:L, :L], start=True, stop=False), [("STm", par), vkey], [("bank", 7)])
            P.op("pe", lambda e, L=L, tok0=tok0, col=col, cur=cur: e.matmul(hb[:, col:col + L], lhsT=Sbf[cur][:], rhs=qe[:, tok0:tok0 + L], start=False, stop=True), [("Sbf", cur), "qe"], [("bank", 7)])
            if hgdbg == 4:
                yield
                continue
            P.op("pe", lambda e, L=L, par=par, Vcur=Vcur: e.matmul(bk(1)[:, 0:128], lhsT=keTok[par][:L, :], rhs=Vcur, start=True, stop=True), [("keTok", par), vkey], [("bank", 1)])
            if hgdbg == 5:
                yield
                continue
            nxt = 1 - cur
            dcol = dl[:, di:di + 1]
            P.op("dve", lambda e, S_cur=S_cur, dcol=dcol: e.tensor_scalar(out=S1[:], in0=S_cur[:], scalar1=dcol, scalar2=None, op0=ALU.mult), [S_key, "dl"], ["S1"])
            P.op("dve", lambda e, nxt=nxt, dcol=dcol: e.scalar_tensor_tensor(out=Sst[nxt][:], in0=bk(1)[:, 0:128], scalar=dcol, in1=S1[:], op0=ALU.mult, op1=ALU.add), [("bank", 1), "S1", "dl"], [("S", nxt)])
            if kind == "p":
                P.op("dve", lambda e, nxt=nxt: e.tensor_copy(out=Sbf[nxt][:], in_=Sst[nxt][:]), [("S", nxt)], [("Sbf", nxt)])
                if ci == 31:
                    P.dma("sp", nsp[h, :, :], Sst[nxt][:], [("S", nxt)], [], "nsp")
            else:
                P.dma("sp", nss[ci * 8 + h, :, :], Sst[nxt][:], [("S", nxt)], [], ("nss", nxt))
            cur = nxt
            if hgdbg == 6:
                yield
                continue
            do_out = (kind == "p" and ci % 4 == 3) or (kind == "s" and ci == 15)
            if do_out:
                if kind == "p":
                    n = 256
                    tc0 = 64 * (ci - 3)
                else:
                    n = 64
                    tc0 = 2048
                P.op("act", lambda e, n=n: e.activation(out=hOsb[:, :n], in_=hb[:, 0:n], func=AF.Copy), [("bank", 7)], ["hOsb"])
                P.op("act", lambda e, n=n: e.activation(out=hsq[:, :n], in_=hOsb[:, :n], func=AF.Square), ["hOsb"], ["hsq"])
                P.op("pe", lambda e, n=n: e.matmul(bk(2)[:, :n], lhsT=ones[:], rhs=hsq[:, :n], start=True, stop=True), ["hsq", "ones"], [("bank", 2)])
                P.op("act", lambda e, n=n: e.activation(out=hrs[:, :n], in_=bk(2)[:, :n], func=AF.Ln, bias=EPS, scale=1.0 / 128), [("bank", 2)], ["hrs"])
                P.op("act", lambda e, n=n: e.activation(out=hrs[:, :n], in_=hrs[:, :n], func=AF.Exp, scale=-0.5), ["hrs"], ["hrs"])
                P.op("dve", lambda e, n=n: e.scalar_tensor_tensor(out=hm1[:, :n], in0=hOsb[:, :n], scalar=ghgc[:, h:h + 1], in1=hrs[:, :n], op0=ALU.mult, op1=ALU.mult), ["hOsb", "hrs", "ghgc"], ["hm1"])
                P.op("dve", lambda e, n=n, tc0=tc0: e.tensor_tensor(out=hmix[:, :n], in0=hm1[:, :n], in1=sg[:, tc0:tc0 + n], op=ALU.mult), ["hm1", "sg"], ["hmix"])
                P.dma("sp", mixT[1024 + h * 128:1024 + (h + 1) * 128, tc0:tc0 + n], hmix[:, :n], ["hmix"], [], "hmixo")
            yield

    def run_interleaved(gens):
        gens = list(gens)
        while gens:
            for g in list(gens):
                try:
                    next(g)
                except StopIteration:
                    gens.remove(g)

    for h in range(nheads):
        if stop == "sbonly":
            run_interleaved([sb_head(h)])
        elif stop == "hgonly":
            run_interleaved([hg_head(h)])
        else:
            run_interleaved([sb_head(h), hg_head(h)])
    P.barrier()
    if stop in ("p1b", "sbonly", "hgonly"):
        P.emit()
        return nc
    S_p1b.close()
    S_p1.close()

    S_p1d = contextlib.ExitStack()
    ptb = sb(S_p1d, "ptb", [128, 256], I32)
    iot = sb(S_p1d, "iot", [128, 256], I32)
    idxt = sb(S_p1d, "idxt", [128, 256], I32)
    Kp = [sb(S_p1d, "Kp%d" % i, [128, 4, 1024], BF16) for i in range(2)]
    Vp = [sb(S_p1d, "Vp%d" % i, [128, 16, 1024], BF16) for i in range(2)]
    KTp = [sb(S_p1d, "KTp%d" % i, [128, 1024], BF16) for i in range(2)]
    biasT = sb(S_p1d, "biasT", [128, 512], F32)
    zbt = sb(S_p1d, "zbt", [128, 512], F32)
    et = sb(S_p1d, "et", [128, 512], F32)
    spt = sb(S_p1d, "spt", [128, 512], BF16)
    suf = sb(S_p1d, "suf", [128, 512], F32)
    t1s = sb(S_p1d, "t1s", [128, 512], F32)
    Ws = sb(S_p1d, "Ws", [128, 512], BF16)
    zbn = sb(S_p1d, "zbn", [4, 32], F32)
    en = sb(S_p1d, "en", [4, 32], F32)
    spn = sb(S_p1d, "spn", [4, 32], F32)
    spnm = sb(S_p1d, "spnm", [4, 32], BF16)
    t1n = sb(S_p1d, "t1n", [4, 32], F32)
    Wn = sb(S_p1d, "Wn", [4, 32], F32)
    Wnm = sb(S_p1d, "Wnm", [4, 32], BF16)
    Vn = sb(S_p1d, "Vn", [4, 1024], BF16)
    Oss = sb(S_p1d, "Oss", [4, 1024], F32)
    Oall = sb(S_p1d, "Oall", [64, 1024], F32)
    sqs = sb(S_p1d, "sqs", [64, 1024], F32)
    ssq8 = sb(S_p1d, "ssq8", [64, 8], F32)
    rs8 = sb(S_p1d, "rs8", [64, 8], F32)
    gsb_bc = sb(S_p1d, "gsb_bc", [64, 1024], F32)
    On = sb(S_p1d, "On", [64, 1024], F32)
    Onb = sb(S_p1d, "Onb", [64, 1024], BF16)
    mixs = sb(S_p1d, "mixs", [128, 8, 64], BF16)

    P.dma("sp", ptb[:], pbc(pt, 128), [], ["ptb"], "ptb")
    P.dma("sp", gsb_bc[:], pbc(sbg, 64), [], ["gsb_bc"], "gsb_bc")
    P.op("pool", lambda e: e.iota(iot[:], pattern=[[0, 256]], base=0, channel_multiplier=1), [], ["iot"])
    P.op("dve", lambda e: e.scalar_tensor_tensor(out=idxt[:], in0=ptb[:], scalar=128.0, in1=iot[:], op0=ALU.mult, op1=ALU.add), ["ptb", "iot"], ["idxt"])
    P.op("pe", lambda e: e.matmul(bk(0)[:, :], lhsT=ones32[:], rhs=hselb[:], start=True, stop=True), [], [("bank", 0)])
    P.op("act", lambda e: e.activation(out=biasT[:], in_=bk(0)[:, :], func=AF.Copy), [("bank", 0)], ["biasT"])

    for b in range(16):
        vp = b % 2
        for pg in range(4):
            kp = (b * 4 + pg) % 2
            for jj in range(4):
                j = pg * 4 + jj
                col = b * 16 + j
                P.op("pool", lambda e, kp=kp, jj=jj, col=col: e.indirect_dma_start(out=Kp[kp][:, jj, :], out_offset=None, in_=ck, in_offset=bass.IndirectOffsetOnAxis(ap=idxt[:, col:col + 1], axis=0)),
                     ["idxt"], [("Kp", kp, jj)], dsem=("Kp", kp))
                P.op("pool", lambda e, vp=vp, j=j, col=col: e.indirect_dma_start(out=Vp[vp][:, j, :], out_offset=None, in_=cv, in_offset=bass.IndirectOffsetOnAxis(ap=idxt[:, col:col + 1], axis=0)),
                     ["idxt"], [("Vp", vp, j)], dsem=("Vp", vp))
            for jj in range(4):
                j = pg * 4 + jj
                tp = j % 2
                pv = bkbf(2 + tp)
                for hh in range(8):
                    P.op("pe", lambda e, kp=kp, jj=jj, hh=hh, pv=pv: e.transpose(out=pv[:, hh * 128:(hh + 1) * 128], in_=Kp[kp][:, jj, hh * 128:(hh + 1) * 128], identity=ident[:, :]),
                         [("Kp", kp, q_) for q_ in range(4)] + ["ident"], [("bank", 2 + tp)])
                if tp == 0:
                    P.op("act", lambda e, tp=tp, pv=pv: e.activation(out=KTp[tp][:, :], in_=pv[:, 0:1024], func=AF.Copy), [("bank", 2 + tp)], [("KTp", tp)])
                else:
                    P.op("dve", lambda e, tp=tp, pv=pv: e.tensor_copy(out=KTp[tp][:, :], in_=pv[:, 0:1024]), [("bank", 2 + tp)], [("KTp", tp)])
                for hh in range(8):
                    zc = (j * 8 + hh) * 4
                    P.op("pe", lambda e, tp=tp, hh=hh, zc=zc, b=b: e.matmul(bk(4)[:, zc:zc + 4], lhsT=KTp[tp][:, hh * 128:(hh + 1) * 128], rhs=QTs[:, hh, 4 * b:4 * b + 4], start=True, stop=True),
                         [("KTp", tp)], [("bank", 4)])
        for hh in range(8):
            P.op("pe", lambda e, hh=hh, b=b: e.matmul(bk(5)[:4, hh * 4:hh * 4 + 4], lhsT=KTs[:, hh, 4 * b:4 * b + 4], rhs=QTs[:, hh, 4 * b:4 * b + 4], start=True, stop=True), [], [("bank", 5)])
        P.op("dve", lambda e: e.tensor_tensor(out=zbt[:], in0=bk(4)[:, :], in1=biasT[:], op=ALU.add), [("bank", 4), "biasT"], ["zbt"])
        P.op("act", lambda e: e.activation(out=et[:], in_=zbt[:], func=AF.Exp), ["zbt"], ["et"])
        P.op("act", lambda e: e.activation(out=spt[:], in_=et[:], func=AF.Ln, bias=1.0), ["et"], ["spt"])
        P.op("dve", lambda e: e.tensor_tensor(out=zbn[:], in0=bk(5)[:4, 0:32], in1=biasT[:4, 0:32], op=ALU.add), [("bank", 5), "biasT"], ["zbn"])
        P.op("act", lambda e: e.activation(out=en[:], in_=zbn[:], func=AF.Exp), ["zbn"], ["en"])
        P.op("act", lambda e: e.activation(out=spn[:], in_=en[:], func=AF.Ln, bias=1.0), ["en"], ["spn"])
        P.op("dve", lambda e: e.tensor_tensor(out=spnm[:], in0=spn[:], in1=m4[:], op=ALU.mult), ["spn", "m4"], ["spnm"])
        P.op("pe", lambda e: e.matmul(bk(6)[:, :], lhsT=triu[:], rhs=spt[:], start=True, stop=True), ["spt", "triu"], [("bank", 6)])
        P.op("pe", lambda e: e.matmul(bk(7)[:, :], lhsT=ones[:], rhs=spt[:], start=True, stop=True), ["spt", "ones"], [("bank", 7)])
        P.op("pe", lambda e: e.matmul(bk(5)[:, 64:96], lhsT=ones[:4, :], rhs=spnm[:], start=True, stop=True), ["spnm", "ones", "zbn"], [("bank", 5)])
        P.op("pe", lambda e: e.matmul(bk(5)[:4, 128:160], lhsT=triu[:4, :4], rhs=spnm[:], start=True, stop=True), ["spnm", "triu", "zbn"], [("bank", 5)])
        P.op("dve", lambda e: e.tensor_copy(out=suf[:, 480:512], in_=bk(5)[:, 64:96]), [("bank", 5)], ["suf"])
        for j in range(14, -1, -1):
            P.op("dve", lambda e, j=j: e.tensor_tensor(out=suf[:, j * 32:(j + 1) * 32], in0=suf[:, (j + 1) * 32:(j + 2) * 32], in1=bk(7)[:, (j + 1) * 32:(j + 2) * 32], op=ALU.add), ["suf", ("bank", 7)], ["suf"])
        P.op("dve", lambda e: e.tensor_tensor(out=t1s[:], in0=zbt[:], in1=spt[:], op=ALU.subtract), ["zbt", "spt"], ["t1s"])
        P.op("dve", lambda e: e.tensor_tensor(out=t1s[:], in0=t1s[:], in1=bk(6)[:, :], op=ALU.subtract), ["t1s", ("bank", 6)], ["t1s"])
        P.op("dve", lambda e: e.tensor_tensor(out=t1s[:], in0=t1s[:], in1=suf[:], op=ALU.subtract), ["t1s", "suf"], ["t1s"])
        P.op("act", lambda e: e.activation(out=Ws[:], in_=t1s[:], func=AF.Exp), ["t1s"], ["Ws"])
        P.op("dve", lambda e: e.tensor_tensor(out=t1n[:], in0=zbn[:], in1=spn[:], op=ALU.subtract), ["zbn", "spn"], ["t1n"])
        P.op("dve", lambda e: e.tensor_tensor(out=t1n[:], in0=t1n[:], in1=bk(5)[:4, 128:160], op=ALU.subtract), ["t1n", ("bank", 5)], ["t1n"])
        P.op("act", lambda e: e.activation(out=Wn[:], in_=t1n[:], func=AF.Exp), ["t1n"], ["Wn"])
        P.op("dve", lambda e: e.tensor_tensor(out=Wnm[:], in0=Wn[:], in1=m4[:], op=ALU.mult), ["Wn", "m4"], ["Wnm"])
        P.dma("sp", Vn[0:4, :], Vs_all[4 * b:4 * b + 4, :, :].rearrange("p h d -> p (h d)"), [], ["Vn"], "Vn")
        for hh in range(8):
            ob = hh // 4
            oc = (hh % 4) * 128
            for j in range(16):
                zc = (j * 8 + hh) * 4
                P.op("pe", lambda e, hh=hh, j=j, zc=zc, ob=ob, oc=oc, vp=vp: e.matmul(bk(ob)[:4, oc:oc + 128], lhsT=Ws[:, zc:zc + 4], rhs=Vp[vp][:, j, hh * 128:(hh + 1) * 128], start=(j == 0), stop=False, skip_group_check=True),
                     ["Ws"] + [("Vp", vp, q_) for q_ in range(16)], [("bank", ob)])
            P.op("pe", lambda e, hh=hh, ob=ob, oc=oc: e.matmul(bk(ob)[:4, oc:oc + 128], lhsT=Wnm[:4, hh * 4:hh * 4 + 4], rhs=Vn[0:4, hh * 128:(hh + 1) * 128], start=False, stop=True, skip_group_check=True),
                 ["Wnm", "Vn"], [("bank", ob)])
        P.op("act", lambda e: e.activation(out=Oss[:, 0:512], in_=bk(0)[:4, :], func=AF.Copy), [("bank", 0)], ["Oss"])
        P.op("act", lambda e: e.activation(out=Oss[:, 512:1024], in_=bk(1)[:4, :], func=AF.Copy), [("bank", 1)], ["Oss"])
        P.dma("sp", Oall[4 * b:4 * b + 4, :], Oss[:, :], ["Oss"], ["Oall"], ("Oall", b % 2))
    P.op("act", lambda e: e.activation(out=sqs[:], in_=Oall[:], func=AF.Square), ["Oall"], ["sqs"])
    P.op("dve", lambda e: e.tensor_reduce(out=ssq8[:], in_=sqs[:].rearrange("p (h d) -> p h d", d=128), axis=AX.X, op=ALU.add), ["sqs"], ["ssq8"])
    P.op("act", lambda e: e.activation(out=rs8[:], in_=ssq8[:], func=AF.Ln, bias=EPS, scale=1.0 / 128), ["ssq8"], ["rs8"])
    P.op("act", lambda e: e.activation(out=rs8[:], in_=rs8[:], func=AF.Exp, scale=-0.5), ["rs8"], ["rs8"])
    for hh in range(8):
        P.op("dve", lambda e, hh=hh: e.scalar_tensor_tensor(out=Onb[:, hh * 128:(hh + 1) * 128], in0=Oall[:, hh * 128:(hh + 1) * 128], scalar=rs8[:, hh:hh + 1], in1=gsb_bc[:, hh * 128:(hh + 1) * 128], op0=ALU.mult, op1=ALU.mult),
             ["Oall", "rs8", "gsb_bc"], ["Onb"])
    pv = bkbf(2)
    for hh in range(8):
        P.op("pe", lambda e, hh=hh: e.transpose(out=pv[:, hh * 64:(hh + 1) * 64], in_=Onb[:64, hh * 128:(hh + 1) * 128], identity=ident[:64, :64]), ["Onb", "ident"], [("bank", 2)])
    P.op("dve", lambda e: e.tensor_copy(out=mixs[:, :, :].rearrange("p h t -> p (h t)"), in_=pv[:, 0:512]), [("bank", 2)], ["mixs"])
    for hh in range(8):
        P.dma("sp", mixT[hh * 128:(hh + 1) * 128, 2048:2112], mixs[:, hh, :], ["mixs"], [], ("mixs_o", hh % 2))
    P.barrier()
    S_p1d.close()
    if stop == "p1d":
        P.emit()
        return nc

    S_p23 = contextlib.ExitStack()
    NSLOT = 6
    hbuf = sb(S_p23, "hbuf", [128, NSLOT, D], F32)
    hnT = sb(S_p23, "hnT", [128, 16, 768], BF16)
    w2 = sb(S_p23, "w2", [128, 4, 8192], BF16)
    mixt = [sb(S_p23, "mixt%d" % i, [128, 16, 128], BF16) for i in range(2)]
    g2bc = sb(S_p23, "g2bc", [128, D], F32)
    gfbc = sb(S_p23, "gfbc", [128, D], F32)
    junk2 = sb(S_p23, "junk2", [128, D], BF16)
    hs = sb(S_p23, "hs", [128, D], BF16)
    yst = sb(S_p23, "yst", [128, D], F32)
    rt = [sb(S_p23, "rt%d" % i, [128, 512], BF16) for i in range(2)]
    aT = [sb(S_p23, "aT%d" % i, [128, 4, 768], BF16) for i in range(2)]
    P.dma("sp", g2bc[:], pbc(n2g, 128), [], ["gbc"], "gbc2")
    P.dma("sp", gfbc[:], pbc(fg, 128), [], ["gfbc"], "gfbc")
    mixT_v = mixT.rearrange("(c p) t -> p c t", p=128)
    w2cnt = [0]

    def w2_unit_up(ffg):
        u = w2cnt[0] % 4
        w2cnt[0] += 1
        src = w_up[:, ffg * 512:(ffg + 1) * 512].rearrange("(c p) n -> p c n", p=128)
        P.dma("pool", w2[:, u, :].rearrange("p (c n) -> p c n", c=16), src, [], [("w2", u)], ("w2", u))
        return u

    def w2_unit_down(ffg):
        u = w2cnt[0] % 4
        w2cnt[0] += 1
        src = w_down[ffg * 512:(ffg + 1) * 512, :].rearrange("(f p) n -> p f n", p=128)
        P.dma("pool", w2[:, u, :].rearrange("p (f n) -> p f n", f=4), src, [], [("w2", u)], ("w2", u))
        return u

    passes = [list(range(0, 6)), list(range(6, 12)), list(range(12, 17))]
    for pi, pblks in enumerate(passes):
        p0 = BLKS[pblks[0]][0]
        ntok = sum(BLKS[b][1] for b in pblks)
        tgs = []
        o = 0
        while o < ntok:
            n = min(512, ntok - o)
            tgs.append((o, n))
            o += n
        w2cnt[0] = 0
        for cg in range(4):
            src = w_out[:, cg * 512:(cg + 1) * 512].rearrange("(c p) n -> p c n", p=128)
            P.dma("pool", w2[:, cg, :].rearrange("p (c n) -> p c n", c=16), src, [], [("w2", cg)], ("w2", cg))
        for si, bi in enumerate(pblks):
            r0, n = BLKS[bi]
            mp = si % 2
            P.dma("sp", hbuf[:n, si, :], x[r0:r0 + n, :], [], [("hb", si)], ("hbld", si % 3))
            P.dma("sp", mixt[mp][:, :, :n], mixT_v[:, :, r0:r0 + n], [], [("mixt", mp)], ("mixt", mp))
            for cg in range(4):
                bnk = cg % 2
                w2v = w2[:, cg, :].rearrange("p (c n) -> p c n", c=16)
                for c in range(16):
                    P.op("pe", lambda e, c=c, n=n, mp=mp, bnk=bnk, w2v=w2v: e.matmul(bk(bnk)[:n, :], lhsT=mixt[mp][:, c, :n], rhs=w2v[:, c, :], start=(c == 0), stop=(c == 15)),
                         [("mixt", mp), ("w2", cg)], [("bank", bnk)])
                P.op("dve", lambda e, n=n, si=si, cg=cg, bnk=bnk: e.tensor_tensor(out=hbuf[:n, si, cg * 512:(cg + 1) * 512], in0=hbuf[:n, si, cg * 512:(cg + 1) * 512], in1=bk(bnk)[:n, :], op=ALU.add),
                     [("hb", si), ("bank", bnk)], [("hb", si)])
            norm_transpose(hbuf[:n, si, :], n, g2bc, hnT, r0 - p0, junk2, hs, ("hb", si), 2, dkey="hnT")
        for ffg in range(16):
            uu = w2_unit_up(ffg)
            ud = w2_unit_down(ffg)
            ap_ = ffg % 2
            wu = w2[:, uu, :].rearrange("p (c n) -> p c n", c=16)
            wd = w2[:, ud, :].rearrange("p (f n) -> p f n", f=4)
            k = 0
            for (lt0, n) in tgs:
                for fc in range(4):
                    bnk = k % 2
                    rp = k % 2
                    k += 1
                    for c in range(16):
                        P.op("pe", lambda e, c=c, fc=fc, lt0=lt0, n=n, bnk=bnk, wu=wu: e.matmul(bk(bnk)[:, :n], lhsT=wu[:, c, fc * 128:(fc + 1) * 128], rhs=hnT[:, c, lt0:lt0 + n], start=(c == 0), stop=(c == 15)),
                             [("w2", uu), ("hnT", 0), ("hnT", 1)], [("bank", bnk)])
                    P.op("act", lambda e, n=n, bnk=bnk, rp=rp: e.activation(out=rt[rp][:, :n], in_=bk(bnk)[:, :n], func=AF.Relu), [("bank", bnk)], [("rt", rp)])
                    P.op("pool", lambda e, n=n, rp=rp, fc=fc, lt0=lt0, ap_=ap_: e.tensor_tensor(out=aT[ap_][:, fc, lt0:lt0 + n], in0=rt[rp][:, :n], in1=rt[rp][:, :n], op=ALU.mult), [("rt", rp)], [("aT", ap_)])
            for si, bi in enumerate(pblks):
                r0, n = BLKS[bi]
                lc0 = r0 - p0
                for cg in range(4):
                    bnk = 4 + (cg % 2)
                    for fc in range(4):
                        P.op("pe", lambda e, fc=fc, n=n, lc0=lc0, cg=cg, bnk=bnk, ap_=ap_, wd=wd: e.matmul(bk(bnk)[:n, :], lhsT=aT[ap_][:, fc, lc0:lc0 + n], rhs=wd[:, fc, cg * 512:(cg + 1) * 512], start=(fc == 0), stop=(fc == 3)),
                             [("aT", ap_), ("w2", ud)], [("bank", bnk)])
                    P.op("dve", lambda e, n=n, si=si, cg=cg, bnk=bnk: e.tensor_tensor(out=hbuf[:n, si, cg * 512:(cg + 1) * 512], in0=hbuf[:n, si, cg * 512:(cg + 1) * 512], in1=bk(bnk)[:n, :], op=ALU.add),
                         [("hb", si), ("bank", bnk)], [("hb", si)])
        for si, bi in enumerate(pblks):
            r0, n = BLKS[bi]
            P.op("act", lambda e, n=n, si=si: e.activation(out=junk2[:n, :], in_=hbuf[:n, si, :], func=AF.Square, accum_out=sscol[:n, 0:1]), [("hb", si)], ["junk", "ss0"])
            rstd_col(n, "ss0", sscol[:n, 2:3], 1.0 / D)
            P.op("dve", lambda e, n=n, si=si: e.scalar_tensor_tensor(out=yst[:n, :], in0=hbuf[:n, si, :], scalar=sscol[:n, 2:3], in1=gfbc[:n, :], op0=ALU.mult, op1=ALU.mult), [("hb", si), "rstdc", "gfbc"], ["yst"])
            P.dma("sp", y[r0:r0 + n, :], yst[:n, :], ["yst"], [], "y_out")
        P.barrier()
    S_p23.close()
    P.emit()
    S_all.close()
    return nc


def _consts():
    bf = ml_dtypes.bfloat16
    i = np.arange(128)
    c = {}
    c["c_ident"] = np.eye(128, dtype=np.float32).astype(bf)
    c["c_triu"] = (i[:, None] > i[None, :]).astype(np.float32).astype(bf)
    c["c_omt"] = (i[:, None] <= i[None, :]).astype(np.float32).astype(bf)
    c["c_ones"] = np.ones((128, 128), np.float32).astype(bf)
    c["c_mstrict"] = (i[:, None] < i[None, :]).astype(np.float32).astype(bf)
    i6 = np.arange(64)
    c["c_mincl"] = (i6[:, None] <= i6[None, :]).astype(np.float32).astype(bf)
    sc = np.ones((1, T), np.float32)
    sc[0, 0:2048:64] = 0.0
    sc[0, 2048:2112:4] = 0.0
    c["c_scan"] = sc
    m4 = np.zeros((4, 8, 4), np.float32)
    for ii in range(4):
        for t in range(4):
            m4[ii, :, t] = 1.0 if ii < t else 0.0
    c["c_m4"] = m4.reshape(4, 32)
    hs = np.zeros((8, 16, 8, 4), np.float32)
    for h in range(8):
        hs[h, :, h, :] = 1.0
    c["c_hsel"] = hs.reshape(8, 512)
    return c


_NC_CACHE = {}


def kernel(x_prompt, x_sample, cache_k, cache_v, state_hgrn, page_table, norm1_g, w_in, sb_bias,
           sb_norm_g, hg_norm_g, hg_lb_logits, w_out, norm2_g, w_up, w_down, final_norm_g):
    f32 = np.float32
    n_rows = cache_k.shape[1] * cache_k.shape[2]
    if n_rows not in _NC_CACHE:
        _NC_CACHE[n_rows] = build_program(n_rows)
    nc = _NC_CACHE[n_rows]
    ncores = 8
    ckf = np.ascontiguousarray(np.asarray(cache_k, f32).reshape(n_rows, 1024))
    cvf = np.ascontiguousarray(np.asarray(cache_v, f32).reshape(n_rows, 1024))
    consts = _consts()
    shared = {
        "ck": ckf, "cv": cvf,
        "w_in": np.ascontiguousarray(np.asarray(w_in, f32)[0]),
        "w_out": np.ascontiguousarray(np.asarray(w_out, f32)[0]),
        "w_up": np.ascontiguousarray(np.asarray(w_up, f32)[0]),
        "w_down": np.ascontiguousarray(np.asarray(w_down, f32)[0]),
        "n1g": np.asarray(norm1_g, f32).reshape(1, D),
        "n2g": np.asarray(norm2_g, f32).reshape(1, D),
        "fg": np.asarray(final_norm_g, f32).reshape(1, D),
        "sbb": np.asarray(sb_bias, f32).reshape(1, 8),
        "sbg": np.asarray(sb_norm_g, f32).reshape(1, 1024),
        "hgg": np.asarray(hg_norm_g, f32).reshape(1, 1024),
        "lbl": np.ascontiguousarray(np.asarray(hg_lb_logits, f32)),
    }
    shared.update(consts)
    xs_ = np.asarray(x_sample, f32)
    xp_ = np.asarray(x_prompt, f32)
    in_maps = []
    for c in range(ncores):
        m = dict(shared)
        m["x"] = np.ascontiguousarray(np.concatenate([xp_[c], xs_[16 * c:16 * c + 16].reshape(64, D)], axis=0))
        m["st"] = np.ascontiguousarray(np.asarray(state_hgrn, f32)[0, 16 * c:16 * c + 16].reshape(128, 128, 128))
        m["pt"] = np.ascontiguousarray(np.asarray(page_table, np.int32)[16 * c:16 * c + 16].reshape(1, 256))
        in_maps.append(m)
    res = run_bass_kernel_spmd(nc, in_maps, core_ids=list(range(ncores)))
    R = res.results
    y_prompt = np.stack([R[c]["y"][:2048] for c in range(ncores)]).astype(f32)
    y_sample = np.concatenate([R[c]["y"][2048:].reshape(16, 4, D) for c in range(ncores)], axis=0).astype(f32)
    nkp = np.stack([R[c]["nk"][:2048].reshape(2048, 8, 128) for c in range(ncores)])[None].astype(f32)
    nvp = np.stack([R[c]["nv"][:2048].reshape(2048, 8, 128) for c in range(ncores)])[None].astype(f32)
    nsp_ = np.stack([R[c]["nsp"] for c in range(ncores)])[None].astype(f32)
    nks = np.concatenate([R[c]["nk"][2048:].reshape(16, 4, 8, 128) for c in range(ncores)], axis=0)[None].astype(f32)
    nvs = np.concatenate([R[c]["nv"][2048:].reshape(16, 4, 8, 128) for c in range(ncores)], axis=0)[None].astype(f32)
    nss_ = np.concatenate([R[c]["nss"].reshape(16, 8, 128, 128) for c in range(ncores)], axis=0)[None].astype(f32)
    return (y_prompt, y_sample, nkp, nvp, nsp_, nks, nvs, nss_)
```

```python
import contextlib
import numpy as np
import ml_dtypes
import concourse.bass as bass
import concourse.mybir as mybir
from concourse.bass_utils import run_bass_kernel_spmd

F32 = mybir.dt.float32
BF16 = mybir.dt.bfloat16
I32 = mybir.dt.int32
AF = mybir.ActivationFunctionType
ALU = mybir.AluOpType
AX = mybir.AxisListType

COMPUTE = ("pe", "act", "dve", "pool")
T = 2112
D = 2048
EPS = 1e-6
N_PHYS_ROWS = 2560 * 128


class Op:
    __slots__ = ("eng", "fn", "reads", "writes", "dsem", "deps", "signal", "sigval", "idx", "is_dma")

    def __init__(self, eng, fn, reads, writes, dsem):
        self.eng = eng
        self.fn = fn
        self.reads = reads
        self.writes = writes
        self.dsem = dsem
        self.is_dma = dsem is not None
        self.deps = []
        self.signal = False
        self.sigval = 0


class Prog:
    def __init__(self, nc):
        self.nc = nc
        self.ops = []
        self.last_w = {}
        self.readers = {}
        self.bar_start = 0

    def op(self, eng, fn, reads=(), writes=(), dsem=None):
        o = Op(eng, fn, tuple(reads), tuple(writes), dsem)
        o.idx = len(self.ops)
        deps = set()
        raw = set()
        for r in o.reads:
            w = self.last_w.get(r)
            if w is not None:
                deps.add(w)
                raw.add(w)
        for wk in o.writes:
            w = self.last_w.get(wk)
            if w is not None:
                deps.add(w)
            for rd in self.readers.get(wk, ()):
                deps.add(rd)
        fdeps = []
        for d in deps:
            po = self.ops[d]
            if (not po.is_dma) and (not o.is_dma) and po.eng == o.eng and o.eng == "pe":
                continue
            fdeps.append(d)
        o.deps = fdeps
        self.ops.append(o)
        for r in o.reads:
            self.readers.setdefault(r, []).append(o.idx)
        for wk in o.writes:
            self.last_w[wk] = o.idx
            self.readers[wk] = []
        return o

    def dma(self, q, out, in_, reads, writes, sem, **kw):
        return self.op(q, lambda e: e.dma_start(out=out, in_=in_, **kw), reads, writes, dsem=sem)

    def barrier(self):
        last = {}
        dmas = {}
        for o in self.ops[self.bar_start:]:
            if o.fn is None:
                continue
            if o.is_dma:
                dmas[o.dsem] = o.idx
            else:
                last[o.eng] = o.idx
        deps = list(dmas.values()) + list(last.values())
        for eng in ("pe", "act", "dve", "pool", "sp"):
            o = Op(eng, None, (), (), None)
            o.idx = len(self.ops)
            o.deps = list(deps)
            self.ops.append(o)
        self.bar_start = len(self.ops)
        self.last_w.clear()
        self.readers.clear()

    def emit(self):
        nc = self.nc
        ops = self.ops
        for o in ops:
            best = {}
            keep = []
            for d in o.deps:
                po = ops[d]
                if po.is_dma:
                    keep.append(d)
                else:
                    if po.eng not in best or best[po.eng] < d:
                        best[po.eng] = d
            o.deps = keep + list(best.values())
            for d in o.deps:
                ops[d].signal = True
        cnt = {e: 0 for e in COMPUTE}
        dcnt = {}
        for o in ops:
            if o.is_dma:
                dcnt[o.dsem] = dcnt.get(o.dsem, 0) + 1
                o.sigval = 16 * dcnt[o.dsem]
            elif o.signal:
                cnt[o.eng] += 1
                o.sigval = cnt[o.eng]
        stack = contextlib.ExitStack()
        sems = {}
        for e in COMPUTE:
            sems[e] = stack.enter_context(nc.semaphore("s_" + e))
        for k in dcnt:
            sems[("d", k)] = stack.enter_context(nc.semaphore("d%d" % len(sems)))
        self.n_sems = len(sems)
        engmap = {"pe": nc.tensor, "act": nc.scalar, "dve": nc.vector, "pool": nc.gpsimd, "sp": nc.sync}
        by_eng = {e: [] for e in engmap}
        for o in ops:
            by_eng[o.eng].append(o)

        def run_engine(ename, eh):
            known = {}
            for o in by_eng[ename]:
                need = {}
                for d in o.deps:
                    po = ops[d]
                    sk = ("d", po.dsem) if po.is_dma else po.eng
                    if need.get(sk, 0) < po.sigval:
                        need[sk] = po.sigval
                for sk, v in need.items():
                    if known.get(sk, 0) >= v:
                        continue
                    known[sk] = v
                    eh.wait_ge(sems[sk], v)
                if o.fn is None:
                    continue
                ins = o.fn(eh)
                if o.is_dma:
                    ins.then_inc(sems[("d", o.dsem)], 16)
                elif o.signal:
                    ins.then_inc(sems[o.eng], 1)

        with nc.Block() as block:
            @block.sync
            def _(e):
                run_engine("sp", e)

            @block.tensor
            def _(e):
                run_engine("pe", e)

            @block.scalar
            def _(e):
                run_engine("act", e)

            @block.vector
            def _(e):
                run_engine("dve", e)

            @block.gpsimd
            def _(e):
                run_engine("pool", e)
        stack.close()


def build_program(n_cache_rows=N_PHYS_ROWS, stop=None, nheads=8):
    nc = bass.Bass("TRN2", target_bir_lowering=False)
    P = Prog(nc)

    def din(name, shape, dt=F32):
        return nc.dram_tensor(name, list(shape), dt, kind="ExternalInput").ap()

    def dout(name, shape, dt=F32):
        return nc.dram_tensor(name, list(shape), dt, kind="ExternalOutput").ap()

    x = din("x", [T, D])
    ck = din("ck", [n_cache_rows, 1024])
    cv = din("cv", [n_cache_rows, 1024])
    st_in = din("st", [128, 128, 128])
    pt = din("pt", [1, 256], I32)
    w_in = din("w_in", [D, 7168])
    w_out = din("w_out", [D, D])
    w_up = din("w_up", [D, 8192])
    w_down = din("w_down", [8192, D])
    n1g = din("n1g", [1, D])
    n2g = din("n2g", [1, D])
    fg = din("fg", [1, D])
    sbb = din("sbb", [1, 8])
    sbg = din("sbg", [1, 1024])
    hgg = din("hgg", [1, 1024])
    lbl = din("lbl", [2, 1024])
    c_ident = din("c_ident", [128, 128], BF16)
    c_triu = din("c_triu", [128, 128], BF16)
    c_omt = din("c_omt", [128, 128], BF16)
    c_ones = din("c_ones", [128, 128], BF16)
    c_mstrict = din("c_mstrict", [128, 128], BF16)
    c_mincl = din("c_mincl", [64, 64], BF16)
    c_scan = din("c_scan", [1, T])
    c_m4 = din("c_m4", [4, 32])
    c_hsel = din("c_hsel", [8, 512])

    y = dout("y", [T, D])
    nk = dout("nk", [T, 1024])
    nv = dout("nv", [T, 1024])
    nsp = dout("nsp", [8, 128, 128])
    nss = dout("nss", [128, 128, 128])
    mixT = nc.dram_tensor("mixT_scr", [D, T], BF16).ap()

    BLKS = [(i * 128, 128) for i in range(16)] + [(2048, 64)]
    TG = [(0, 512), (512, 512), (1024, 512), (1536, 512), (2048, 64)]

    S_all = contextlib.ExitStack()

    def sb(stack, name, shape, dt):
        return stack.enter_context(nc.sbuf_tensor(name, list(shape), dt))

    banks = [S_all.enter_context(nc.psum_tensor("bank%d" % i, [128, 512], F32)) for i in range(8)]

    def bk(i):
        return banks[i]

    def bkbf(i):
        return banks[i][:].bitcast(BF16)

    ident = sb(S_all, "ident", [128, 128], BF16)
    triu = sb(S_all, "triu", [128, 128], BF16)
    omt = sb(S_all, "omt", [128, 128], BF16)
    ones = sb(S_all, "ones", [128, 128], BF16)
    mstrict = sb(S_all, "mstrict", [128, 128], BF16)
    mincl = sb(S_all, "mincl", [64, 64], BF16)
    m4 = sb(S_all, "m4", [4, 32], F32)
    hsel = sb(S_all, "hsel", [8, 512], BF16)
    hsel32 = sb(S_all, "hsel32", [8, 512], F32)
    biasc = sb(S_all, "biasc", [128, 8], F32)
    bias8 = sb(S_all, "bias8", [8, 1], F32)
    hselb = sb(S_all, "hselb", [8, 512], F32)
    ones32 = sb(S_all, "ones32", [8, 128], F32)
    gsbc = sb(S_all, "gsbc", [128, 8], F32)
    ghgc = sb(S_all, "ghgc", [128, 8], F32)
    lbt = sb(S_all, "lbt", [128, 2, 8], F32)
    lbc = sb(S_all, "lbc", [128, 8], F32)
    omlc = sb(S_all, "omlc", [128, 8], F32)
    sscol = sb(S_all, "sscol", [128, 4], F32)
    QTs = sb(S_all, "QTs", [128, 8, 64], BF16)
    KTs = sb(S_all, "KTs", [128, 8, 64], BF16)
    Vs_all = sb(S_all, "Vs_all", [64, 8, 128], BF16)

    def pbc(ap2d, n):
        return ap2d.partition_broadcast(n).rearrange("p o n -> p (o n)")

    def ld(dst, src, key, **kw):
        P.dma("sp", dst, src, reads=[], writes=[key], sem=key, **kw)

    ld(ident[:], c_ident, "ident")
    ld(triu[:], c_triu, "triu")
    ld(omt[:], c_omt, "omt")
    ld(ones[:], c_ones, "ones")
    ld(mstrict[:], c_mstrict, "mstrict")
    ld(mincl[:], c_mincl, "mincl")
    ld(m4[:], c_m4, "m4")
    ld(hsel32[:], c_hsel, "hsel32")
    ld(biasc[:], pbc(sbb, 128), "biasc")
    ld(bias8[:], sbb.rearrange("o h -> h o"), "bias8", allow_slow_non_contiguous=True)
    ld(gsbc[:], sbg.rearrange("o (h d) -> d (o h)", d=128), "gsbc", allow_slow_non_contiguous=True)
    ld(ghgc[:], hgg.rearrange("o (h d) -> d (o h)", d=128), "ghgc", allow_slow_non_contiguous=True)
    ld(lbt[:], lbl.rearrange("r (h d) -> d r h", d=128), "lbt", allow_slow_non_contiguous=True)
    P.op("dve", lambda e: e.tensor_copy(out=hsel[:], in_=hsel32[:]), ["hsel32"], ["hsel"])
    P.op("dve", lambda e: e.tensor_scalar(out=hselb[:], in0=hsel32[:], scalar1=bias8[:, 0:1], scalar2=None, op0=ALU.mult), ["hsel32", "bias8"], ["hselb"])
    P.op("dve", lambda e: e.memset(ones32[:], 1.0), [], ["ones32"])
    P.op("dve", lambda e: e.tensor_tensor(out=lbc[:], in0=lbt[:, 1, :], in1=lbt[:, 0, :], op=ALU.subtract), ["lbt"], ["lbc"])
    P.op("act", lambda e: e.activation(out=lbc[:], in_=lbc[:], func=AF.Exp), ["lbc"], ["lbc"])
    P.op("dve", lambda e: e.tensor_scalar(out=lbc[:], in0=lbc[:], scalar1=1.0, scalar2=None, op0=ALU.add), ["lbc"], ["lbc"])
    P.op("dve", lambda e: e.reciprocal(out=lbc[:], in_=lbc[:]), ["lbc"], ["lbc"])
    P.op("dve", lambda e: e.tensor_scalar(out=omlc[:], in0=lbc[:], scalar1=-1.0, scalar2=1.0, op0=ALU.mult, op1=ALU.add), ["lbc"], ["omlc"])

    if stop == "p0":
        P.barrier()
        P.emit()
        return nc

    def rstd_col(n, src_key, dst, scale):
        P.op("act", lambda e: e.activation(out=sscol[:n, 1:2], in_=sscol[:n, 0:1], func=AF.Ln, bias=EPS, scale=scale), [src_key], ["ss1"])
        P.op("act", lambda e: e.activation(out=dst, in_=sscol[:n, 1:2], func=AF.Exp, scale=-0.5), ["ss1"], ["rstdc"])

    S_p1 = contextlib.ExitStack()
    xnT = sb(S_p1, "xnT", [128, 16, T], BF16)

    def norm_transpose(src_tile, n, gbc, dstT, c0, junk, xs, src_key, tb0, dkey="xT"):
        P.op("act", lambda e: e.activation(out=junk[:n, :], in_=src_tile, func=AF.Square, accum_out=sscol[:n, 0:1]), [src_key], ["junk", "ss0"])
        import os as _os
        dbg = int(_os.environ.get("DBG", "9"))
        if dbg < 2:
            return
        rstd_col(n, "ss0", sscol[:n, 2:3], 1.0 / D)
        if dbg < 3:
            return
        P.op("dve", lambda e: e.scalar_tensor_tensor(out=xs[:n, :], in0=src_tile, scalar=sscol[:n, 2:3], in1=gbc[:n, :], op0=ALU.mult, op1=ALU.mult), [src_key, "rstdc", "gbc"], ["xs"])
        if dbg < 4:
            return
        for half in range(2):
            bnk = tb0 + half
            pv = bkbf(bnk)
            for cc in range(8):
                c = half * 8 + cc
                P.op("pe", lambda e, c=c, cc=cc, pv=pv: e.transpose(out=pv[:, cc * 128:cc * 128 + n], in_=xs[:n, c * 128:(c + 1) * 128], identity=ident[:n, :n]), ["xs", "ident"], [("bank", bnk)])
            if dbg < 5:
                continue
            if dbg == 5 and half == 1:
                continue
            if dbg == 6 and half == 0:
                continue
            src = pv[:, :].rearrange("p (c t) -> p c t", c=8)[:, :, :n]
            dst = dstT[:, half * 8:half * 8 + 8, c0:c0 + n]
            if half == 0 or dbg == 7:
                P.op("act", lambda e, src=src, dst=dst: e.activation(out=dst, in_=src, func=AF.Copy), [("bank", bnk)], [(dkey, half)])
            else:
                P.op("dve", lambda e, src=src, dst=dst: e.tensor_copy(out=dst, in_=src), [("bank", bnk)], [(dkey, half)])

    S_p1a = contextlib.ExitStack()
    g1bc = sb(S_p1a, "g1bc", [128, D], F32)
    xin = [sb(S_p1a, "xin%d" % i, [128, D], F32) for i in range(2)]
    junk = sb(S_p1a, "junk", [128, D], BF16)
    xs = sb(S_p1a, "xs", [128, D], BF16)
    P.dma("sp", g1bc[:], pbc(n1g, 128), [], ["gbc"], "gbc")
    for bi, (r0, n) in enumerate(BLKS):
        xb = xin[bi % 2]
        P.dma("sp", xb[:n, :], x[r0:r0 + n, :], [], [("xin", bi % 2)], ("xin", bi % 2))
        norm_transpose(xb[:n, :], n, g1bc, xnT, r0, junk, xs, ("xin", bi % 2), 2)
    P.barrier()
    S_p1a.close()
    if stop == "p1a":
        P.emit()
        return nc

    S_p1b = contextlib.ExitStack()
    NU = 8
    wring = sb(S_p1b, "wring", [128, NU, 16, 128], BF16)
    ring_cnt = [0]

    def load_w_unit(col0):
        u = ring_cnt[0] % NU
        ring_cnt[0] += 1
        src = w_in[:, col0:col0 + 128].rearrange("(c p) n -> p c n", p=128)
        P.dma("pool", wring[:, u, :, :], src, [], [("wr", u)], ("wr", u))
        return u

    QT = sb(S_p1b, "QT", [128, T], BF16)
    KT = sb(S_p1b, "KT", [128, T], BF16)
    Kb = sb(S_p1b, "Kb", [128, 17, 128], BF16)
    Vb = sb(S_p1b, "Vb", [128, 17, 128], BF16)
    kvst = [sb(S_p1b, "kvst%d" % i, [128, 256], F32) for i in range(2)]
    e_t = [sb(S_p1b, "e_t%d" % i, [128, 512], F32) for i in range(2)]
    spb_t = [sb(S_p1b, "spb%d" % i, [128, 512], BF16) for i in range(2)]
    t1_t = [sb(S_p1b, "t1_%d" % i, [128, 512], F32) for i in range(2)]
    s_t = [sb(S_p1b, "s_%d" % i, [128, 512], F32) for i in range(2)]
    W_t = [sb(S_p1b, "W_%d" % i, [128, 512], BF16) for i in range(2)]
    Osb = sb(S_p1b, "Osb", [128, 512], F32)
    sq_t = sb(S_p1b, "sq_t", [128, 512], BF16)
    rs_t = sb(S_p1b, "rs_t", [128, 512], F32)
    mix_t = sb(S_p1b, "mix_t", [128, 512], BF16)
    hA = sb(S_p1b, "hA", [128, 512], F32)
    hB = sb(S_p1b, "hB", [128, 512], F32)
    hC = sb(S_p1b, "hC", [128, 512], F32)
    hD = sb(S_p1b, "hD", [128, 512], F32)
    hE = sb(S_p1b, "hE", [128, 512], F32)
    scanm = sb(S_p1b, "scanm", [128, T], BF16)
    scan32 = sb(S_p1b, "scan32", [128, 512], F32)
    qe = sb(S_p1b, "qe", [128, T], BF16)
    ke = sb(S_p1b, "ke", [128, T], BF16)
    sg = sb(S_p1b, "sg", [128, T], BF16)
    hiT = sb(S_p1b, "hiT", [128, T], BF16)
    Vc = sb(S_p1b, "Vc", [64, 33, 128], BF16)
    Vs4 = sb(S_p1b, "Vs4", [4, 16, 128], BF16)
    dl = sb(S_p1b, "dl", [128, 48], F32)
    STm = [sb(S_p1b, "STm%d" % i, [64, 64], BF16) for i in range(2)]
    keTok = [sb(S_p1b, "keTok%d" % i, [64, 128], BF16) for i in range(2)]
    Sst = [sb(S_p1b, "Sst%d" % i, [128, 128], F32) for i in range(2)]
    S1 = sb(S_p1b, "S1", [128, 128], F32)
    Sbf = [sb(S_p1b, "Sbf%d" % i, [128, 128], BF16) for i in range(2)]
    Sld = [sb(S_p1b, "Sld%d" % i, [128, 128], F32) for i in range(2)]
    hOsb = sb(S_p1b, "hOsb", [128, 256], F32)
    hsq = sb(S_p1b, "hsq", [128, 256], BF16)
    hrs = sb(S_p1b, "hrs", [128, 256], F32)
    hm1 = sb(S_p1b, "hm1", [128, 256], F32)
    hmix = sb(S_p1b, "hmix", [128, 256], BF16)

    for gi, (t0, n) in enumerate(TG):
        P.dma("sp", scan32[:, :n], pbc(c_scan[:, t0:t0 + n], 128), [], ["scan32"], "scan32")
        P.op("dve", lambda e, t0=t0, n=n: e.tensor_copy(out=scanm[:, t0:t0 + n], in_=scan32[:, :n]), ["scan32"], ["scanm"])

    INV_SQRT_D = 128.0 ** -0.5

    def proj_fm(u, t0, n, bnk):
        for c in range(16):
            P.op("pe", lambda e, c=c: e.matmul(bk(bnk)[:, :n], lhsT=wring[:, u, c, :], rhs=xnT[:, c, t0:t0 + n], start=(c == 0), stop=(c == 15)),
                 [("wr", u)], [("bank", bnk)])

    def sb_head(h):
        uq = load_w_unit(h * 128)
        if ring_cnt[0] % NU == NU - 1:
            ring_cnt[0] += 1
        uk = load_w_unit(1024 + h * 128)
        uv = load_w_unit(2048 + h * 128)
        assert uv == uk + 1
        for gi, (t0, n) in enumerate(TG):
            bnk = gi % 2
            proj_fm(uq, t0, n, bnk)
            P.op("act", lambda e, t0=t0, n=n, bnk=bnk: e.activation(out=QT[:, t0:t0 + n], in_=bk(bnk)[:, :n], func=AF.Copy, scale=INV_SQRT_D), [("bank", bnk)], ["QT"])
            if gi == 4:
                P.op("dve", lambda e, bnk=bnk: e.tensor_scalar(out=QTs[:, h, :], in0=bk(bnk)[:, :64], scalar1=INV_SQRT_D, scalar2=None, op0=ALU.mult), [("bank", bnk)], ["QTs"])
        for bi, (r0, n) in enumerate(BLKS):
            bnk = bi % 2
            for c in range(16):
                P.op("pe", lambda e, c=c, r0=r0, n=n, bnk=bnk: e.matmul(bk(bnk)[:n, 0:256].rearrange("p (u n) -> p u n", u=2), lhsT=xnT[:, c, r0:r0 + n], rhs=wring[:, uk:uk + 2, c, :], start=(c == 0), stop=(c == 15)),
                     [("wr", uk), ("wr", uv)], [("bank", bnk)])
            kp = bi % 2
            P.op("act", lambda e, n=n, bnk=bnk, kp=kp: e.activation(out=kvst[kp][:n, :], in_=bk(bnk)[:n, 0:256], func=AF.Copy), [("bank", bnk)], [("kvst", kp)])
            P.dma("sp", nk[r0:r0 + n, h * 128:(h + 1) * 128], kvst[kp][:n, 0:128], [("kvst", kp)], [], ("nk", kp))
            P.dma("sp", nv[r0:r0 + n, h * 128:(h + 1) * 128], kvst[kp][:n, 128:256], [("kvst", kp)], [], ("nv", kp))
            P.op("dve", lambda e, n=n, bi=bi, kp=kp: e.tensor_copy(out=Kb[:n, bi, :], in_=kvst[kp][:n, 0:128]), [("kvst", kp)], ["Kb"])
            P.op("dve", lambda e, n=n, bi=bi, kp=kp: e.tensor_copy(out=Vb[:n, bi, :], in_=kvst[kp][:n, 128:256]), [("kvst", kp)], ["Vb"])
            if bi == 16:
                P.op("dve", lambda e, kp=kp: e.tensor_copy(out=Vs_all[:64, h, :], in_=kvst[kp][:64, 128:256]), [("kvst", kp)], ["Vs_all"])
        pv = bkbf(2)
        for g0 in range(0, 17, 8):
            blks = list(range(g0, min(g0 + 8, 17)))
            for bi in blks:
                r0, n = BLKS[bi]
                P.op("pe", lambda e, bi=bi, n=n, g0=g0: e.transpose(out=pv[:, (bi - g0) * 128:(bi - g0) * 128 + n], in_=Kb[:n, bi, :], identity=ident[:n, :n]), ["Kb", "ident"], [("bank", 2)])
            if g0 < 16:
                P.op("dve", lambda e, g0=g0: e.tensor_copy(out=KT[:, g0 * 128:g0 * 128 + 1024], in_=pv[:, 0:1024]), [("bank", 2)], ["KT"])
            else:
                P.op("dve", lambda e: e.tensor_copy(out=KT[:, 2048:2112], in_=pv[:, 0:64]), [("bank", 2)], ["KT"])
                P.op("dve", lambda e: e.tensor_copy(out=KTs[:, h, :], in_=pv[:, 0:64]), [("bank", 2)], ["KTs"])
        yield
        bh = biasc[:, h:h + 1]
        step = 0
        for qg in range(4):
            q0 = qg * 512
            blocks = list(range(4 * qg + 3, -1, -1))
            nb = len(blocks)
            base = step

            def prm(idx, qg=qg, blocks=blocks, base=base):
                kb = blocks[idx]
                j = kb - 4 * qg
                c0 = 128 * j if j >= 0 else 0
                par = (base + idx) % 2
                return kb, j, c0, par

            def front(idx, q0=q0):
                kb, j, c0, par = prm(idx)
                zb = 3 + par
                Z = bk(zb)
                P.op("pe", lambda e: e.matmul(Z[:, c0:512], lhsT=KT[:, kb * 128:(kb + 1) * 128], rhs=QT[:, q0 + c0:q0 + 512], start=True, stop=True),
                     ["KT", "QT"], [("bank", zb)])
                P.op("act", lambda e: e.activation(out=e_t[par][:, c0:], in_=Z[:, c0:512], func=AF.Exp, bias=bh), [("bank", zb), "biasc"], [("e", par)])
                P.op("act", lambda e: e.activation(out=spb_t[par][:, c0:], in_=e_t[par][:, c0:], func=AF.Ln, bias=1.0), [("e", par)], [("spb", par)])
                if j >= 0:
                    P.op("pool", lambda e: e.tensor_tensor(out=spb_t[par][:, c0:c0 + 128], in0=spb_t[par][:, c0:c0 + 128], in1=mstrict[:], op=ALU.mult), [("spb", par), "mstrict"], [("spb", par)])
                P.op("dve", lambda e: e.tensor_tensor(out=t1_t[par][:, c0:], in0=Z[:, c0:512], in1=spb_t[par][:, c0:], op=ALU.subtract), [("bank", zb), ("spb", par)], [("t1", par)])

            def mid(idx):
                kb, j, c0, par = prm(idx)
                P.op("pe", lambda e: e.matmul(bk(5)[:, c0:512], lhsT=triu[:], rhs=spb_t[par][:, c0:], start=(idx == 0), stop=True, skip_group_check=True),
                     [("spb", par), "triu"], [("bank", 5)])
                P.op("dve", lambda e: e.tensor_tensor(out=s_t[par][:, c0:], in0=t1_t[par][:, c0:], in1=bk(5)[:, c0:512], op=ALU.subtract), [("t1", par), ("bank", 5)], [("s", par)])
                P.op("act", lambda e: e.activation(out=W_t[par][:, c0:], in_=s_t[par][:, c0:], func=AF.Exp, bias=bh), [("s", par), "biasc"], [("W", par)])
                if j >= 0:
                    P.op("pool", lambda e: e.tensor_tensor(out=W_t[par][:, c0:c0 + 128], in0=W_t[par][:, c0:c0 + 128], in1=mstrict[:], op=ALU.mult), [("W", par), "mstrict"], [("W", par)])

            def back(idx, nb=nb):
                kb, j, c0, par = prm(idx)
                if idx < nb - 1:
                    P.op("pe", lambda e: e.matmul(bk(5)[:, c0:512], lhsT=omt[:], rhs=spb_t[par][:, c0:], start=False, stop=True, skip_group_check=True),
                         [("spb", par), "omt"], [("bank", 5)])
                P.op("pe", lambda e: e.matmul(bk(6)[:, c0:512], lhsT=Vb[:, kb, :], rhs=W_t[par][:, c0:], start=(idx == 0), stop=(idx == nb - 1), skip_group_check=True),
                     [("W", par), "Vb"], [("bank", 6)])

            front(0)
            for idx in range(nb):
                if idx + 1 < nb:
                    front(idx + 1)
                mid(idx)
                yield
                back(idx)
            step += nb
            P.op("act", lambda e: e.activation(out=Osb[:], in_=bk(6)[:, :], func=AF.Copy), [("bank", 6)], ["Osb"])
            P.op("act", lambda e: e.activation(out=sq_t[:], in_=Osb[:], func=AF.Square), ["Osb"], ["sq"])
            P.op("pe", lambda e: e.matmul(bk(2)[:, :], lhsT=ones[:], rhs=sq_t[:], start=True, stop=True), ["sq", "ones"], [("bank", 2)])
            P.op("act", lambda e: e.activation(out=rs_t[:], in_=bk(2)[:, :], func=AF.Ln, bias=EPS, scale=1.0 / 128), [("bank", 2)], ["rs"])
            P.op("act", lambda e: e.activation(out=rs_t[:], in_=rs_t[:], func=AF.Exp, scale=-0.5), ["rs"], ["rs"])
            P.op("dve", lambda e: e.scalar_tensor_tensor(out=mix_t[:], in0=Osb[:], scalar=gsbc[:, h:h + 1], in1=rs_t[:], op0=ALU.mult, op1=ALU.mult), ["Osb", "rs", "gsbc"], ["mix"])
            P.dma("sp", mixT[h * 128:(h + 1) * 128, q0:q0 + 512], mix_t[:], ["mix"], [], "mixo")
            yield

    def silu_from_bank(bnk, n, out_ap, out_key):
        P.op("act", lambda e: e.activation(out=hD[:, :n], in_=bk(bnk)[:, :n], func=AF.Exp, scale=-1.0), [("bank", bnk)], ["hD"])
        P.op("act", lambda e: e.activation(out=hE[:, :n], in_=bk(bnk)[:, :n], func=AF.Copy), [("bank", bnk)], ["hE"])
        P.op("dve", lambda e: e.tensor_scalar(out=hD[:, :n], in0=hD[:, :n], scalar1=1.0, scalar2=None, op0=ALU.add), ["hD"], ["hD"])
        P.op("dve", lambda e: e.reciprocal(out=hD[:, :n], in_=hD[:, :n]), ["hD"], ["hD"])
        P.op("dve", lambda e: e.tensor_tensor(out=out_ap, in0=hE[:, :n], in1=hD[:, :n], op=ALU.mult), ["hD", "hE"], [out_key])

    def hg_head(h):
        uq = load_w_unit(3072 + h * 128)
        uf = load_w_unit(4096 + h * 128)
        ui = load_w_unit(5120 + h * 128)
        ug = load_w_unit(6144 + h * 128)
        for gi, (t0, n) in enumerate(TG):
            proj_fm(uq, t0, n, 0)
            silu_from_bank(0, n, hC[:, :n], "hC")
            proj_fm(uf, t0, n, 1)
            P.op("act", lambda e, n=n: e.activation(out=hA[:, :n], in_=bk(1)[:, :n], func=AF.Exp, scale=-1.0), [("bank", 1)], ["hA"])
            P.op("dve", lambda e, n=n: e.tensor_scalar(out=hA[:, :n], in0=hA[:, :n], scalar1=1.0, scalar2=None, op0=ALU.add), ["hA"], ["hA"])
            P.op("dve", lambda e, n=n: e.reciprocal(out=hA[:, :n], in_=hA[:, :n]), ["hA"], ["hA"])
            P.op("dve", lambda e, n=n: e.tensor_scalar(out=hA[:, :n], in0=hA[:, :n], scalar1=omlc[:, h:h + 1], scalar2=lbc[:, h:h + 1], op0=ALU.mult, op1=ALU.add), ["hA", "omlc", "lbc"], ["hA"])
            P.op("act", lambda e, n=n: e.activation(out=hB[:, :n], in_=hA[:, :n], func=AF.Ln), ["hA"], ["hB"])
            P.op("dve", lambda e, n=n: e.tensor_scalar(out=hA[:, :n], in0=hA[:, :n], scalar1=-1.0, scalar2=1.0, op0=ALU.mult, op1=ALU.add), ["hA"], ["hA"])
            P.op("dve", lambda e, n=n, t0=t0: e.tensor_tensor_scan(out=hD[:, :n], data0=scanm[:, t0:t0 + n], data1=hB[:, :n], initial=0.0, op0=ALU.mult, op1=ALU.add), ["hB", "scanm"], ["hD"])
            P.op("act", lambda e, n=n: e.activation(out=hB[:, :n], in_=hD[:, :n], func=AF.Exp), ["hD"], ["hB"])
            P.op("dve", lambda e, n=n, t0=t0: e.tensor_tensor(out=qe[:, t0:t0 + n], in0=hC[:, :n], in1=hB[:, :n], op=ALU.mult), ["hC", "hB"], ["qe"])
            if gi < 4:
                P.op("dve", lambda e, gi=gi, n=n: e.tensor_copy(out=dl[:, gi * 8:gi * 8 + 8], in_=hB[:, :n].rearrange("p (c l) -> p c l", l=64)[:, :, 63]), ["hB"], ["dl"])
            else:
                P.op("dve", lambda e, n=n: e.tensor_copy(out=dl[:, 32:48], in_=hB[:, :n].rearrange("p (c l) -> p c l", l=4)[:, :, 3]), ["hB"], ["dl"])
            P.op("act", lambda e, n=n: e.activation(out=hB[:, :n], in_=hD[:, :n], func=AF.Exp, scale=-1.0), ["hD"], ["hB"])
            P.op("dve", lambda e, n=n, t0=t0: e.tensor_tensor(out=ke[:, t0:t0 + n], in0=hA[:, :n], in1=hB[:, :n], op=ALU.mult), ["hA", "hB"], ["ke"])
            proj_fm(ug, t0, n, 0)
            silu_from_bank(0, n, sg[:, t0:t0 + n], "sg")
        for gi, (t0, n) in enumerate(TG):
            bnk = gi % 2
            proj_fm(ui, t0, n, bnk)
            P.op("act", lambda e, t0=t0, n=n, bnk=bnk: e.activation(out=hiT[:, t0:t0 + n], in_=bk(bnk)[:, :n], func=AF.Copy), [("bank", bnk)], ["hiT"])
        pvh = bkbf(2)
        for g0 in range(0, 33, 8):
            cis = list(range(g0, min(g0 + 8, 33)))
            for ci in cis:
                P.op("pe", lambda e, ci=ci, g0=g0: e.transpose(out=pvh[:64, (ci - g0) * 128:(ci - g0 + 1) * 128], in_=hiT[:, 64 * ci:64 * ci + 64], identity=ident[:, :]), ["hiT", "ident"], [("bank", 2)])
            k = len(cis)
            P.op("dve", lambda e, g0=g0, k=k: e.tensor_copy(out=Vc[:64, g0:g0 + k, :], in_=pvh[:64, 0:k * 128].rearrange("p (c d) -> p c d", d=128)), [("bank", 2)], ["Vc"])
        for b in range(16):
            P.dma("sp", Vs4[0:4, b, :], Vc[4 * b:4 * b + 4, 32, :], ["Vc"], ["Vs4"], ("vs4", b % 4))
        yield
        import os as _os
        hgdbg = int(_os.environ.get("HGDBG", "9"))
        if hgdbg < 1:
            return
        P.op("dve", lambda e: e.memset(Sst[0][:], 0.0), [], [("S", 0)])
        P.op("dve", lambda e: e.memset(Sbf[0][:], 0.0), [], [("Sbf", 0)])
        chunks = [("p", ci, 64 * ci, 64, ci) for ci in range(32)] + [("s", b, 2048 + 4 * b, 4, 32 + b) for b in range(16)]
        if hgdbg >= 1 and hgdbg < 9:
            chunks = chunks[:32]
        cur = 0
        hb = bk(7)
        hbb = bkbf(7)
        def hg_front(stp):
            kind, ci, tok0, L, di = chunks[stp]
            par = stp % 2
            P.op("pe", lambda e: e.matmul(bk(0)[:L, 0:L], lhsT=ke[:, tok0:tok0 + L], rhs=qe[:, tok0:tok0 + L], start=True, stop=True), ["ke", "qe"], [("bank", 0)])
            P.op("pe", lambda e: e.transpose(out=bkbf(0)[:L, 640:768], in_=ke[:, tok0:tok0 + L], identity=ident[:, :]), ["ke", "ident"], [("bank", 0)])
            P.op("dve", lambda e: e.tensor_tensor(out=STm[par][:L, :L], in0=bk(0)[:L, 0:L], in1=mincl[:L, :L], op=ALU.mult), [("bank", 0), "mincl"], [("STm", par)])
            P.op("dve", lambda e: e.tensor_copy(out=keTok[par][:L, :], in_=bkbf(0)[:L, 640:768]), [("bank", 0)], [("keTok", par)])
            if kind == "s":
                lp = ci % 2
                P.dma("sp", Sld[lp][:], st_in[ci * 8 + h, :, :], [], [("Sld", lp)], ("Sld", lp))

        hg_front(0)
        for stp, (kind, ci, tok0, L, di) in enumerate(chunks):
            par = stp % 2
            if stp + 1 < len(chunks):
                hg_front(stp + 1)
            if kind == "s":
                lp = ci % 2
                S_cur = Sld[lp]
                S_key = ("Sld", lp)
                P.op("dve", lambda e, S_cur=S_cur, cur=cur: e.tensor_copy(out=Sbf[cur][:], in_=S_cur[:]), [S_key], [("Sbf", cur)])
                Vcur = Vs4[0:4, ci, :]
                vkey = "Vs4"
                col = 4 * ci
            else:
                S_cur = Sst[cur]
                S_key = ("S", cur)
                Vcur = Vc[:64, ci, :]
                vkey = "Vc"
                col = (ci % 4) * 64
            P.op("pe", lambda e, L=L, par=par, Vcur=Vcur, col=col: e.matmul(hb[:, col:col + L], lhsT=Vcur, rhs=STm[par][:L, :L], start=True, stop=False), [("STm", par), vkey], [("bank", 7)])
            P.op("pe", lambda e, L=L, tok0=tok0, col=col, cur=cur: e.matmul(hb[:, col:col + L], lhsT=Sbf[cur][:], rhs=qe[:, tok0:tok0 + L], start=False, stop=True), [("Sbf", cur), "qe"], [("bank", 7)])
            if hgdbg == 4:
                yield
                continue
            P.op("pe", lambda e, L=L, par=par, Vcur=Vcur: e.matmul(bk(1)[:, 0:128], lhsT=keTok[par][:L, :], rhs=Vcur, start=True, stop=True), [("keTok", par), vkey], [("bank", 1)])
            if hgdbg == 5:
                yield
                continue
            nxt = 1 - cur
            dcol = dl[:, di:di + 1]
            P.op("dve", lambda e, S_cur=S_cur, dcol=dcol: e.tensor_scalar(out=S1[:], in0=S_cur[:], scalar1=dcol, scalar2=None, op0=ALU.mult), [S_key, "dl"], ["S1"])
            P.op("dve", lambda e, nxt=nxt, dcol=dcol: e.scalar_tensor_tensor(out=Sst[nxt][:], in0=bk(1)[:, 0:128], scalar=dcol, in1=S1[:], op0=ALU.mult, op1=ALU.add), [("bank", 1), "S1", "dl"], [("S", nxt)])
            if kind == "p":
                P.op("dve", lambda e, nxt=nxt: e.tensor_copy(out=Sbf[nxt][:], in_=Sst[nxt][:]), [("S", nxt)], [("Sbf", nxt)])
                if ci == 31:
                    P.dma("sp", nsp[h, :, :], Sst[nxt][:], [("S", nxt)], [], "nsp")
            else:
                P.dma("sp", nss[ci * 8 + h, :, :], Sst[nxt][:], [("S", nxt)], [], ("nss", nxt))
            cur = nxt
            if hgdbg == 6:
                yield
                continue
            do_out = (kind == "p" and ci % 4 == 3) or (kind == "s" and ci == 15)
            if do_out:
                if kind == "p":
                    n = 256
                    tc0 = 64 * (ci - 3)
                else:
                    n = 64
                    tc0 = 2048
                P.op("act", lambda e, n=n: e.activation(out=hOsb[:, :n], in_=hb[:, 0:n], func=AF.Copy), [("bank", 7)], ["hOsb"])
                P.op("act", lambda e, n=n: e.activation(out=hsq[:, :n], in_=hOsb[:, :n], func=AF.Square), ["hOsb"], ["hsq"])
                P.op("pe", lambda e, n=n: e.matmul(bk(2)[:, :n], lhsT=ones[:], rhs=hsq[:, :n], start=True, stop=True), ["hsq", "ones"], [("bank", 2)])
                P.op("act", lambda e, n=n: e.activation(out=hrs[:, :n], in_=bk(2)[:, :n], func=AF.Ln, bias=EPS, scale=1.0 / 128), [("bank", 2)], ["hrs"])
                P.op("act", lambda e, n=n: e.activation(out=hrs[:, :n], in_=hrs[:, :n], func=AF.Exp, scale=-0.5), ["hrs"], ["hrs"])
                P.op("dve", lambda e, n=n: e.scalar_tensor_tensor(out=hm1[:, :n], in0=hOsb[:, :n], scalar=ghgc[:, h:h + 1], in1=hrs[:, :n], op0=ALU.mult, op1=ALU.mult), ["hOsb", "hrs", "ghgc"], ["hm1"])
                P.op("dve", lambda e, n=n, tc0=tc0: e.tensor_tensor(out=hmix[:, :n], in0=hm1[:, :n], in1=sg[:, tc0:tc0 + n], op=ALU.mult), ["hm1", "sg"], ["hmix"])
                P.dma("sp", mixT[1024 + h * 128:1024 + (h + 1) * 128, tc0:tc0 + n], hmix[:, :n], ["hmix"], [], "hmixo")
            yield

    def run_interleaved(gens):
        gens = list(gens)
        while gens:
            for g in list(gens):
                try:
                    next(g)
                except StopIteration:
                    gens.remove(g)

    for h in range(nheads):
        if stop == "sbonly":
            run_interleaved([sb_head(h)])
        elif stop == "hgonly":
            run_interleaved([hg_head(h)])
        else:
            run_interleaved([sb_head(h), hg_head(h)])
    P.barrier()
    if stop in ("p1b", "sbonly", "hgonly"):
        P.emit()
        return nc
    S_p1b.close()
    S_p1.close()

    S_p1d = contextlib.ExitStack()
    ptb = sb(S_p1d, "ptb", [128, 256], I32)
    iot = sb(S_p1d, "iot", [128, 256], I32)
    idxt = sb(S_p1d, "idxt", [128, 256], I32)
    Kp = [sb(S_p1d, "Kp%d" % i, [128, 4, 1024], BF16) for i in range(2)]
    Vp = [sb(S_p1d, "Vp%d" % i, [128, 16, 1024], BF16) for i in range(2)]
    KTp = [sb(S_p1d, "KTp%d" % i, [128, 1024], BF16) for i in range(2)]
    biasT = sb(S_p1d, "biasT", [128, 512], F32)
    zbt = sb(S_p1d, "zbt", [128, 512], F32)
    et = sb(S_p1d, "et", [128, 512], F32)
    spt = sb(S_p1d, "spt", [128, 512], BF16)
    suf = sb(S_p1d, "suf", [128, 512], F32)
    t1s = sb(S_p1d, "t1s", [128, 512], F32)
    Ws = sb(S_p1d, "Ws", [128, 512], BF16)
    zbn = sb(S_p1d, "zbn", [4, 32], F32)
    en = sb(S_p1d, "en", [4, 32], F32)
    spn = sb(S_p1d, "spn", [4, 32], F32)
    spnm = sb(S_p1d, "spnm", [4, 32], BF16)
    t1n = sb(S_p1d, "t1n", [4, 32], F32)
    Wn = sb(S_p1d, "Wn", [4, 32], F32)
    Wnm = sb(S_p1d, "Wnm", [4, 32], BF16)
    Vn = sb(S_p1d, "Vn", [4, 1024], BF16)
    Oss = sb(S_p1d, "Oss", [4, 1024], F32)
    Oall = sb(S_p1d, "Oall", [64, 1024], F32)
    sqs = sb(S_p1d, "sqs", [64, 1024], F32)
    ssq8 = sb(S_p1d, "ssq8", [64, 8], F32)
    rs8 = sb(S_p1d, "rs8", [64, 8], F32)
    gsb_bc = sb(S_p1d, "gsb_bc", [64, 1024], F32)
    On = sb(S_p1d, "On", [64, 1024], F32)
    Onb = sb(S_p1d, "Onb", [64, 1024], BF16)
    mixs = sb(S_p1d, "mixs", [128, 8, 64], BF16)

    P.dma("sp", ptb[:], pbc(pt, 128), [], ["ptb"], "ptb")
    P.dma("sp", gsb_bc[:], pbc(sbg, 64), [], ["gsb_bc"], "gsb_bc")
    P.op("pool", lambda e: e.iota(iot[:], pattern=[[0, 256]], base=0, channel_multiplier=1), [], ["iot"])
    P.op("dve", lambda e: e.scalar_tensor_tensor(out=idxt[:], in0=ptb[:], scalar=128.0, in1=iot[:], op0=ALU.mult, op1=ALU.add), ["ptb", "iot"], ["idxt"])
    P.op("pe", lambda e: e.matmul(bk(0)[:, :], lhsT=ones32[:], rhs=hselb[:], start=True, stop=True), [], [("bank", 0)])
    P.op("act", lambda e: e.activation(out=biasT[:], in_=bk(0)[:, :], func=AF.Copy), [("bank", 0)], ["biasT"])

    for b in range(16):
        vp = b % 2
        for pg in range(4):
            kp = (b * 4 + pg) % 2
            for jj in range(4):
                j = pg * 4 + jj
                col = b * 16 + j
                P.op("pool", lambda e, kp=kp, jj=jj, col=col: e.indirect_dma_start(out=Kp[kp][:, jj, :], out_offset=None, in_=ck, in_offset=bass.IndirectOffsetOnAxis(ap=idxt[:, col:col + 1], axis=0)),
                     ["idxt"], [("Kp", kp, jj)], dsem=("Kp", kp))
                P.op("pool", lambda e, vp=vp, j=j, col=col: e.indirect_dma_start(out=Vp[vp][:, j, :], out_offset=None, in_=cv, in_offset=bass.IndirectOffsetOnAxis(ap=idxt[:, col:col + 1], axis=0)),
                     ["idxt"], [("Vp", vp, j)], dsem=("Vp", vp))
            for jj in range(4):
                j = pg * 4 + jj
                tp = j % 2
                pv = bkbf(2 + tp)
                for hh in range(8):
                    P.op("pe", lambda e, kp=kp, jj=jj, hh=hh, pv=pv: e.transpose(out=pv[:, hh * 128:(hh + 1) * 128], in_=Kp[kp][:, jj, hh * 128:(hh + 1) * 128], identity=ident[:, :]),
                         [("Kp", kp, q_) for q_ in range(4)] + ["ident"], [("bank", 2 + tp)])
                if tp == 0:
                    P.op("act", lambda e, tp=tp, pv=pv: e.activation(out=KTp[tp][:, :], in_=pv[:, 0:1024], func=AF.Copy), [("bank", 2 + tp)], [("KTp", tp)])
                else:
                    P.op("dve", lambda e, tp=tp, pv=pv: e.tensor_copy(out=KTp[tp][:, :], in_=pv[:, 0:1024]), [("bank", 2 + tp)], [("KTp", tp)])
                for hh in range(8):
                    zc = (j * 8 + hh) * 4
                    P.op("pe", lambda e, tp=tp, hh=hh, zc=zc, b=b: e.matmul(bk(4)[:, zc:zc + 4], lhsT=KTp[tp][:, hh * 128:(hh + 1) * 128], rhs=QTs[:, hh, 4 * b:4 * b + 4], start=True, stop=True),
                         [("KTp", tp)], [("bank", 4)])
        for hh in range(8):
            P.op("pe", lambda e, hh=hh, b=b: e.matmul(bk(5)[:4, hh * 4:hh * 4 + 4], lhsT=KTs[:, hh, 4 * b:4 * b + 4], rhs=QTs[:, hh, 4 * b:4 * b + 4], start=True, stop=True), [], [("bank", 5)])
        P.op("dve", lambda e: e.tensor_tensor(out=zbt[:], in0=bk(4)[:, :], in1=biasT[:], op=ALU.add), [("bank", 4), "biasT"], ["zbt"])
        P.op("act", lambda e: e.activation(out=et[:], in_=zbt[:], func=AF.Exp), ["zbt"], ["et"])
        P.op("act", lambda e: e.activation(out=spt[:], in_=et[:], func=AF.Ln, bias=1.0), ["et"], ["spt"])
        P.op("dve", lambda e: e.tensor_tensor(out=zbn[:], in0=bk(5)[:4, 0:32], in1=biasT[:4, 0:32], op=ALU.add), [("bank", 5), "biasT"], ["zbn"])
        P.op("act", lambda e: e.activation(out=en[:], in_=zbn[:], func=AF.Exp), ["zbn"], ["en"])
        P.op("act", lambda e: e.activation(out=spn[:], in_=en[:], func=AF.Ln, bias=1.0), ["en"], ["spn"])
        P.op("dve", lambda e: e.tensor_tensor(out=spnm[:], in0=spn[:], in1=m4[:], op=ALU.mult), ["spn", "m4"], ["spnm"])
        P.op("pe", lambda e: e.matmul(bk(6)[:, :], lhsT=triu[:], rhs=spt[:], start=True, stop=True), ["spt", "triu"], [("bank", 6)])
        P.op("pe", lambda e: e.matmul(bk(7)[:, :], lhsT=ones[:], rhs=spt[:], start=True, stop=True), ["spt", "ones"], [("bank", 7)])
        P.op("pe", lambda e: e.matmul(bk(5)[:, 64:96], lhsT=ones[:4, :], rhs=spnm[:], start=True, stop=True), ["spnm", "ones", "zbn"], [("bank", 5)])
        P.op("pe", lambda e: e.matmul(bk(5)[:4, 128:160], lhsT=triu[:4, :4], rhs=spnm[:], start=True, stop=True), ["spnm", "triu", "zbn"], [("bank", 5)])
        P.op("dve", lambda e: e.tensor_copy(out=suf[:, 480:512], in_=bk(5)[:, 64:96]), [("bank", 5)], ["suf"])
        for j in range(14, -1, -1):
            P.op("dve", lambda e, j=j: e.tensor_tensor(out=suf[:, j * 32:(j + 1) * 32], in0=suf[:, (j + 1) * 32:(j + 2) * 32], in1=bk(7)[:, (j + 1) * 32:(j + 2) * 32], op=ALU.add), ["suf", ("bank", 7)], ["suf"])
        P.op("dve", lambda e: e.tensor_tensor(out=t1s[:], in0=zbt[:], in1=spt[:], op=ALU.subtract), ["zbt", "spt"], ["t1s"])
        P.op("dve", lambda e: e.tensor_tensor(out=t1s[:], in0=t1s[:], in1=bk(6)[:, :], op=ALU.subtract), ["t1s", ("bank", 6)], ["t1s"])
        P.op("dve", lambda e: e.tensor_tensor(out=t1s[:], in0=t1s[:], in1=suf[:], op=ALU.subtract), ["t1s", "suf"], ["t1s"])
        P.op("act", lambda e: e.activation(out=Ws[:], in_=t1s[:], func=AF.Exp), ["t1s"], ["Ws"])
        P.op("dve", lambda e: e.tensor_tensor(out=t1n[:], in0=zbn[:], in1=spn[:], op=ALU.subtract), ["zbn", "spn"], ["t1n"])
        P.op("dve", lambda e: e.tensor_tensor(out=t1n[:], in0=t1n[:], in1=bk(5)[:4, 128:160], op=ALU.subtract), ["t1n", ("bank", 5)], ["t1n"])
        P.op("act", lambda e: e.activation(out=Wn[:], in_=t1n[:], func=AF.Exp), ["t1n"], ["Wn"])
        P.op("dve", lambda e: e.tensor_tensor(out=Wnm[:], in0=Wn[:], in1=m4[:], op=ALU.mult), ["Wn", "m4"], ["Wnm"])
        P.dma("sp", Vn[0:4, :], Vs_all[4 * b:4 * b + 4, :, :].rearrange("p h d -> p (h d)"), [], ["Vn"], "Vn")
        for hh in range(8):
            ob = hh // 4
            oc = (hh % 4) * 128
            for j in range(16):
                zc = (j * 8 + hh) * 4
                P.op("pe", lambda e, hh=hh, j=j, zc=zc, ob=ob, oc=oc, vp=vp: e.matmul(bk(ob)[:4, oc:oc + 128], lhsT=Ws[:, zc:zc + 4], rhs=Vp[vp][:, j, hh * 128:(hh + 1) * 128], start=(j == 0), stop=False, skip_group_check=True),
                     ["Ws"] + [("Vp", vp, q_) for q_ in range(16)], [("bank", ob)])
            P.op("pe", lambda e, hh=hh, ob=ob, oc=oc: e.matmul(bk(ob)[:4, oc:oc + 128], lhsT=Wnm[:4, hh * 4:hh * 4 + 4], rhs=Vn[0:4, hh * 128:(hh + 1) * 128], start=False, stop=True, skip_group_check=True),
                 ["Wnm", "Vn"], [("bank", ob)])
        P.op("act", lambda e: e.activation(out=Oss[:, 0:512], in_=bk(0)[:4, :], func=AF.Copy), [("bank", 0)], ["Oss"])
        P.op("act", lambda e: e.activation(out=Oss[:, 512:1024], in_=bk(1)[:4, :], func=AF.Copy), [("bank", 1)], ["Oss"])
        P.dma("sp", Oall[4 * b:4 * b + 4, :], Oss[:, :], ["Oss"], ["Oall"], ("Oall", b % 2))
    P.op("act", lambda e: e.activation(out=sqs[:], in_=Oall[:], func=AF.Square), ["Oall"], ["sqs"])
    P.op("dve", lambda e: e.tensor_reduce(out=ssq8[:], in_=sqs[:].rearrange("p (h d) -> p h d", d=128), axis=AX.X, op=ALU.add), ["sqs"], ["ssq8"])
    P.op("act", lambda e: e.activation(out=rs8[:], in_=ssq8[:], func=AF.Ln, bias=EPS, scale=1.0 / 128), ["ssq8"], ["rs8"])
    P.op("act", lambda e: e.activation(out=rs8[:], in_=rs8[:], func=AF.Exp, scale=-0.5), ["rs8"], ["rs8"])
    for hh in range(8):
        P.op("dve", lambda e, hh=hh: e.scalar_tensor_tensor(out=Onb[:, hh * 128:(hh + 1) * 128], in0=Oall[:, hh * 128:(hh + 1) * 128], scalar=rs8[:, hh:hh + 1], in1=gsb_bc[:, hh * 128:(hh + 1) * 128], op0=ALU.mult, op1=ALU.mult),
             ["Oall", "rs8", "gsb_bc"], ["Onb"])
    pv = bkbf(2)
    for hh in range(8):
        P.op("pe", lambda e, hh=hh: e.transpose(out=pv[:, hh * 64:(hh + 1) * 64], in_=Onb[:64, hh * 128:(hh + 1) * 128], identity=ident[:64, :64]), ["Onb", "ident"], [("bank", 2)])
    P.op("dve", lambda e: e.tensor_copy(out=mixs[:, :, :].rearrange("p h t -> p (h t)"), in_=pv[:, 0:512]), [("bank", 2)], ["mixs"])
    for hh in range(8):
        P.dma("sp", mixT[hh * 128:(hh + 1) * 128, 2048:2112], mixs[:, hh, :], ["mixs"], [], ("mixs_o", hh % 2))
    P.barrier()
    S_p1d.close()
    if stop == "p1d":
        P.emit()
        return nc

    S_p23 = contextlib.ExitStack()
    NSLOT = 6
    hbuf = sb(S_p23, "hbuf", [128, NSLOT, D], F32)
    hnT = sb(S_p23, "hnT", [128, 16, 768], BF16)
    w2 = sb(S_p23, "w2", [128, 4, 8192], BF16)
    mixt = [sb(S_p23, "mixt%d" % i, [128, 16, 128], BF16) for i in range(2)]
    g2bc = sb(S_p23, "g2bc", [128, D], F32)
    gfbc = sb(S_p23, "gfbc", [128, D], F32)
    junk2 = sb(S_p23, "junk2", [128, D], BF16)
    hs = sb(S_p23, "hs", [128, D], BF16)
    yst = sb(S_p23, "yst", [128, D], F32)
    rt = [sb(S_p23, "rt%d" % i, [128, 512], BF16) for i in range(2)]
    aT = [sb(S_p23, "aT%d" % i, [128, 4, 768], BF16) for i in range(2)]
    P.dma("sp", g2bc[:], pbc(n2g, 128), [], ["gbc"], "gbc2")
    P.dma("sp", gfbc[:], pbc(fg, 128), [], ["gfbc"], "gfbc")
    mixT_v = mixT.rearrange("(c p) t -> p c t", p=128)
    w2cnt = [0]

    def w2_unit_up(ffg):
        u = w2cnt[0] % 4
        w2cnt[0] += 1
        src = w_up[:, ffg * 512:(ffg + 1) * 512].rearrange("(c p) n -> p c n", p=128)
        P.dma("pool", w2[:, u, :].rearrange("p (c n) -> p c n", c=16), src, [], [("w2", u)], ("w2", u))
        return u

    def w2_unit_down(ffg):
        u = w2cnt[0] % 4
        w2cnt[0] += 1
        src = w_down[ffg * 512:(ffg + 1) * 512, :].rearrange("(f p) n -> p f n", p=128)
        P.dma("pool", w2[:, u, :].rearrange("p (f n) -> p f n", f=4), src, [], [("w2", u)], ("w2", u))
        return u

    passes = [list(range(0, 6)), list(range(6, 12)), list(range(12, 17))]
    for pi, pblks in enumerate(passes):
        p0 = BLKS[pblks[0]][0]
        ntok = sum(BLKS[b][1] for b in pblks)
        tgs = []
        o = 0
        while o < ntok:
            n = min(512, ntok - o)
            tgs.append((o, n))
            o += n
        w2cnt[0] = 0
        for cg in range(4):
            src = w_out[:, cg * 512:(cg + 1) * 512].rearrange("(c p) n -> p c n", p=128)
            P.dma("pool", w2[:, cg, :].rearrange("p (c n) -> p c n", c=16), src, [], [("w2", cg)], ("w2", cg))
        for si, bi in enumerate(pblks):
            r0, n = BLKS[bi]
            mp = si % 2
            P.dma("sp", hbuf[:n, si, :], x[r0:r0 + n, :], [], [("hb", si)], ("hbld", si % 3))
            P.dma("sp", mixt[mp][:, :, :n], mixT_v[:, :, r0:r0 + n], [], [("mixt", mp)], ("mixt", mp))
            for cg in range(4):
                bnk = cg % 2
                w2v = w2[:, cg, :].rearrange("p (c n) -> p c n", c=16)
                for c in range(16):
                    P.op("pe", lambda e, c=c, n=n, mp=mp, bnk=bnk, w2v=w2v: e.matmul(bk(bnk)[:n, :], lhsT=mixt[mp][:, c, :n], rhs=w2v[:, c, :], start=(c == 0), stop=(c == 15)),
                         [("mixt", mp), ("w2", cg)], [("bank", bnk)])
                P.op("dve", lambda e, n=n, si=si, cg=cg, bnk=bnk: e.tensor_tensor(out=hbuf[:n, si, cg * 512:(cg + 1) * 512], in0=hbuf[:n, si, cg * 512:(cg + 1) * 512], in1=bk(bnk)[:n, :], op=ALU.add),
                     [("hb", si), ("bank", bnk)], [("hb", si)])
            norm_transpose(hbuf[:n, si, :], n, g2bc, hnT, r0 - p0, junk2, hs, ("hb", si), 2, dkey="hnT")
        for ffg in range(16):
            uu = w2_unit_up(ffg)
            ud = w2_unit_down(ffg)
            ap_ = ffg % 2
            wu = w2[:, uu, :].rearrange("p (c n) -> p c n", c=16)
            wd = w2[:, ud, :].rearrange("p (f n) -> p f n", f=4)
            k = 0
            for (lt0, n) in tgs:
                for fc in range(4):
                    bnk = k % 2
                    rp = k % 2
                    k += 1
                    for c in range(16):
                        P.op("pe", lambda e, c=c, fc=fc, lt0=lt0, n=n, bnk=bnk, wu=wu: e.matmul(bk(bnk)[:, :n], lhsT=wu[:, c, fc * 128:(fc + 1) * 128], rhs=hnT[:, c, lt0:lt0 + n], start=(c == 0), stop=(c == 15)),
                             [("w2", uu), ("hnT", 0), ("hnT", 1)], [("bank", bnk)])
                    P.op("act", lambda e, n=n, bnk=bnk, rp=rp: e.activation(out=rt[rp][:, :n], in_=bk(bnk)[:, :n], func=AF.Relu), [("bank", bnk)], [("rt", rp)])
                    P.op("pool", lambda e, n=n, rp=rp, fc=fc, lt0=lt0, ap_=ap_: e.tensor_tensor(out=aT[ap_][:, fc, lt0:lt0 + n], in0=rt[rp][:, :n], in1=rt[rp][:, :n], op=ALU.mult), [("rt", rp)], [("aT", ap_)])
            for si, bi in enumerate(pblks):
                r0, n = BLKS[bi]
                lc0 = r0 - p0
                for cg in range(4):
                    bnk = 4 + (cg % 2)
                    for fc in range(4):
                        P.op("pe", lambda e, fc=fc, n=n, lc0=lc0, cg=cg, bnk=bnk, ap_=ap_, wd=wd: e.matmul(bk(bnk)[:n, :], lhsT=aT[ap_][:, fc, lc0:lc0 + n], rhs=wd[:, fc, cg * 512:(cg + 1) * 512], start=(fc == 0), stop=(fc == 3)),
                             [("aT", ap_), ("w2", ud)], [("bank", bnk)])
                    P.op("dve", lambda e, n=n, si=si, cg=cg, bnk=bnk: e.tensor_tensor(out=hbuf[:n, si, cg * 512:(cg + 1) * 512], in0=hbuf[:n, si, cg * 512:(cg + 1) * 512], in1=bk(bnk)[:n, :], op=ALU.add),
                         [("hb", si), ("bank", bnk)], [("hb", si)])
        for si, bi in enumerate(pblks):
            r0, n = BLKS[bi]
            P.op("act", lambda e, n=n, si=si: e.activation(out=junk2[:n, :], in_=hbuf[:n, si, :], func=AF.Square, accum_out=sscol[:n, 0:1]), [("hb", si)], ["junk", "ss0"])
            rstd_col(n, "ss0", sscol[:n, 2:3], 1.0 / D)
            P.op("dve", lambda e, n=n, si=si: e.scalar_tensor_tensor(out=yst[:n, :], in0=hbuf[:n, si, :], scalar=sscol[:n, 2:3], in1=gfbc[:n, :], op0=ALU.mult, op1=ALU.mult), [("hb", si), "rstdc", "gfbc"], ["yst"])
            P.dma("sp", y[r0:r0 + n, :], yst[:n, :], ["yst"], [], "y_out")
        P.barrier()
    S_p23.close()
    P.emit()
    S_all.close()
    return nc


def _consts():
    bf = ml_dtypes.bfloat16
    i = np.arange(128)
    c = {}
    c["c_ident"] = np.eye(128, dtype=np.float32).astype(bf)
    c["c_triu"] = (i[:, None] > i[None, :]).astype(np.float32).astype(bf)
    c["c_omt"] = (i[:, None] <= i[None, :]).astype(np.float32).astype(bf)
    c["c_ones"] = np.ones((128, 128), np.float32).astype(bf)
    c["c_mstrict"] = (i[:, None] < i[None, :]).astype(np.float32).astype(bf)
    i6 = np.arange(64)
    c["c_mincl"] = (i6[:, None] <= i6[None, :]).astype(np.float32).astype(bf)
    sc = np.ones((1, T), np.float32)
    sc[0, 0:2048:64] = 0.0
    sc[0, 2048:2112:4] = 0.0
    c["c_scan"] = sc
    m4 = np.zeros((4, 8, 4), np.float32)
    for ii in range(4):
        for t in range(4):
            m4[ii, :, t] = 1.0 if ii < t else 0.0
    c["c_m4"] = m4.reshape(4, 32)
    hs = np.zeros((8, 16, 8, 4), np.float32)
    for h in range(8):
        hs[h, :, h, :] = 1.0
    c["c_hsel"] = hs.reshape(8, 512)
    return c


_NC_CACHE = {}


def kernel(x_prompt, x_sample, cache_k, cache_v, state_hgrn, page_table, norm1_g, w_in, sb_bias,
           sb_norm_g, hg_norm_g, hg_lb_logits, w_out, norm2_g, w_up, w_down, final_norm_g):
    f32 = np.float32
    n_rows = cache_k.shape[1] * cache_k.shape[2]
    if n_rows not in _NC_CACHE:
        _NC_CACHE[n_rows] = build_program(n_rows)
    nc = _NC_CACHE[n_rows]
    ncores = 8
    ckf = np.ascontiguousarray(np.asarray(cache_k, f32).reshape(n_rows, 1024))
    cvf = np.ascontiguousarray(np.asarray(cache_v, f32).reshape(n_rows, 1024))
    consts = _consts()
    shared = {
        "ck": ckf, "cv": cvf,
        "w_in": np.ascontiguousarray(np.asarray(w_in, f32)[0]),
        "w_out": np.ascontiguousarray(np.asarray(w_out, f32)[0]),
        "w_up": np.ascontiguousarray(np.asarray(w_up, f32)[0]),
        "w_down": np.ascontiguousarray(np.asarray(w_down, f32)[0]),
        "n1g": np.asarray(norm1_g, f32).reshape(1, D),
        "n2g": np.asarray(norm2_g, f32).reshape(1, D),
        "fg": np.asarray(final_norm_g, f32).reshape(1, D),
        "sbb": np.asarray(sb_bias, f32).reshape(1, 8),
        "sbg": np.asarray(sb_norm_g, f32).reshape(1, 1024),
        "hgg": np.asarray(hg_norm_g, f32).reshape(1, 1024),
        "lbl": np.ascontiguousarray(np.asarray(hg_lb_logits, f32)),
    }
    shared.update(consts)
    xs_ = np.asarray(x_sample, f32)
    xp_ = np.asarray(x_prompt, f32)
    in_maps = []
    for c in range(ncores):
        m = dict(shared)
        m["x"] = np.ascontiguousarray(np.concatenate([xp_[c], xs_[16 * c:16 * c + 16].reshape(64, D)], axis=0))
        m["st"] = np.ascontiguousarray(np.asarray(state_hgrn, f32)[0, 16 * c:16 * c + 16].reshape(128, 128, 128))
        m["pt"] = np.ascontiguousarray(np.asarray(page_table, np.int32)[16 * c:16 * c + 16].reshape(1, 256))
        in_maps.append(m)
    res = run_bass_kernel_spmd(nc, in_maps, core_ids=list(range(ncores)))
    R = res.results
    y_prompt = np.stack([R[c]["y"][:2048] for c in range(ncores)]).astype(f32)
    y_sample = np.concatenate([R[c]["y"][2048:].reshape(16, 4, D) for c in range(ncores)], axis=0).astype(f32)
    nkp = np.stack([R[c]["nk"][:2048].reshape(2048, 8, 128) for c in range(ncores)])[None].astype(f32)
    nvp = np.stack([R[c]["nv"][:2048].reshape(2048, 8, 128) for c in range(ncores)])[None].astype(f32)
    nsp_ = np.stack([R[c]["nsp"] for c in range(ncores)])[None].astype(f32)
    nks = np.concatenate([R[c]["nk"][2048:].reshape(16, 4, 8, 128) for c in range(ncores)], axis=0)[None].astype(f32)
    nvs = np.concatenate([R[c]["nv"][2048:].reshape(16, 4, 8, 128) for c in range(ncores)], axis=0)[None].astype(f32)
    nss_ = np.concatenate([R[c]["nss"].reshape(16, 8, 128, 128) for c in range(ncores)], axis=0)[None].astype(f32)
    return (y_prompt, y_sample, nkp, nvp, nsp_, nks, nvs, nss_)
```
